# Optimizing a Trainium2 kernel written in Bass

```python
import math
import jax, jax.numpy as jnp
from jax import lax
import numpy as np

D_MODEL = 1024
BATCH = 4
SEQ = 8192
DEPTH = 1

MEM_LEN = 256
BRANCH_WIDTH = D_MODEL // 2
POOL_WINDOWS = (2, 4, 8, 16)
POOL_GROUPS = len(POOL_WINDOWS)
POOL_GROUP_DIM = BRANCH_WIDTH // POOL_GROUPS
DIFF_HEADS = 8
DIFF_QK_DIM = D_MODEL // 32
DIFF_V_DIM = 2 * DIFF_QK_DIM
DIFF_QK_WIDTH = DIFF_HEADS * 2 * DIFF_QK_DIM
MEM_HEADS = 4
MEM_HEAD_DIM = BRANCH_WIDTH // MEM_HEADS
N_BRANCH = 3
Q_BLOCK = 128
LN_EPS = 1e-5
RMS_EPS = 1e-5
DEEPNORM_ALPHA = (2.0 * DEPTH) ** 0.25
DEEPNORM_BETA = (8.0 * DEPTH) ** -0.25
IN_WIDTHS = (BRANCH_WIDTH, BRANCH_WIDTH,
             DIFF_QK_WIDTH, DIFF_QK_WIDTH,
             BRANCH_WIDTH, BRANCH_WIDTH,
             BRANCH_WIDTH, BRANCH_WIDTH,
             N_BRANCH * D_MODEL)
IN_TOTAL = sum(IN_WIDTHS)

kernel_name = "hybrid_pool_diffattn_memxattn_deepnorm"


def layer_norm(x, g, b):
    xf = x.astype(jnp.float32)
    mu = jnp.mean(xf, axis=-1, keepdims=True)
    var = jnp.mean(jnp.square(xf - mu), axis=-1, keepdims=True)
    y = (xf - mu) * lax.rsqrt(var + LN_EPS) * g.astype(jnp.float32) + b.astype(jnp.float32)
    return y.astype(x.dtype)


def rms_norm(x, g):
    xf = x.astype(jnp.float32)
    y = xf * lax.rsqrt(jnp.mean(jnp.square(xf), axis=-1, keepdims=True) + RMS_EPS)
    return (y * g.astype(jnp.float32)).astype(x.dtype)


def lambda_init_fn(layer_idx):
    return 0.8 - 0.6 * math.exp(-0.3 * layer_idx)


def alibi_slopes(n_heads):
    return jnp.array([2.0 ** (-8.0 * (h + 1) / n_heads) for h in range(n_heads)], dtype=jnp.float32)


def causal_multiscale_pool(u):
    b, s, _, c = u.shape
    t_count = jnp.arange(1, s + 1, dtype=jnp.float32)
    outs = []
    for g, w in enumerate(POOL_WINDOWS):
        ug = u[:, :, g, :].astype(jnp.float32)
        csum = jnp.concatenate([jnp.zeros((b, 1, c), jnp.float32), jnp.cumsum(ug, axis=1)], axis=1)
        upper = csum[:, 1:]
        lower = jnp.concatenate([jnp.zeros((b, w - 1, c), jnp.float32), csum[:, : s - w + 1]], axis=1)
        count = jnp.minimum(t_count, float(w))[None, :, None]
        outs.append((upper - lower) / count - ug)
    return jnp.stack(outs, axis=2).astype(u.dtype)


def differential_attention(q, k, v, lam):
    b, s, h, _, dq = q.shape
    n_blk = s // Q_BLOCK
    scale = dq ** -0.5
    slopes = alibi_slopes(h)
    k_pos = jnp.arange(s)
    q_blocks = q.reshape(b, n_blk, Q_BLOCK, h, 2, dq).transpose(1, 0, 2, 3, 4, 5)

    def one_block(args):
        q_blk, i = args
        q_pos = i * Q_BLOCK + jnp.arange(Q_BLOCK)
        sc = jnp.einsum('bqhcd,bkhcd->bhcqk', q_blk, k,
                        preferred_element_type=jnp.float32) * scale
        dist = q_pos[:, None] - k_pos[None, :]
        bias = -slopes[:, None, None] * dist.astype(jnp.float32)[None]
        sc = jnp.where((dist >= 0)[None, None, None], sc + bias[None, :, None], -jnp.inf)
        p = jax.nn.softmax(sc, axis=-1)
        a = p[:, :, 0] - lam * p[:, :, 1]
        return jnp.einsum('bhqk,bkhe->bqhe', a.astype(v.dtype), v)

    out = lax.map(one_block, (q_blocks, jnp.arange(n_blk)))
    return out.transpose(1, 0, 2, 3, 4).reshape(b, s, h, v.shape[-1])


def memory_cross_attention(q, mk, mv):
    scale = q.shape[-1] ** -0.5
    sc = jnp.einsum('bshd,bmhd->bhsm', q, mk, preferred_element_type=jnp.float32) * scale
    p = jax.nn.softmax(sc, axis=-1)
    return jnp.einsum('bhsm,bmhd->bshd', p.astype(mv.dtype), mv)


def hybrid_layer(x, mem, layer_idx, w_in, b_gate, pool_w, pool_scale, lambda_q1, lambda_k1,
                 lambda_q2, lambda_k2, diff_norm_g, w_mem_kv, w_branch, w_out, ln_out_g, ln_out_b):
    b, s, d = x.shape
    proj = jnp.einsum('bsd,de->bse', x, w_in)
    offs = [0]
    for wdt in IN_WIDTHS:
        offs.append(offs[-1] + wdt)
    parts = [proj[..., offs[i]:offs[i + 1]] for i in range(len(IN_WIDTHS))]
    pool_u, pool_z, dq_, dk_, dv_, diff_z, mq_, mem_z, gates = parts

    u = pool_u.reshape(b, s, POOL_GROUPS, POOL_GROUP_DIM)
    pooled = causal_multiscale_pool(u)
    pooled = jnp.einsum('bsgc,gce->bsge', pooled, pool_w).reshape(b, s, BRANCH_WIDTH) * pool_scale
    o_pool = pooled * jax.nn.silu(pool_z)

    lam_init = lambda_init_fn(layer_idx)
    lam = (jnp.exp(jnp.sum(lambda_q1.astype(jnp.float32) * lambda_k1.astype(jnp.float32)))
           - jnp.exp(jnp.sum(lambda_q2.astype(jnp.float32) * lambda_k2.astype(jnp.float32)))
           + lam_init)
    q = dq_.reshape(b, s, DIFF_HEADS, 2, DIFF_QK_DIM)
    k = dk_.reshape(b, s, DIFF_HEADS, 2, DIFF_QK_DIM)
    v = dv_.reshape(b, s, DIFF_HEADS, DIFF_V_DIM)
    att = differential_attention(q, k, v, lam)
    att = rms_norm(att, diff_norm_g) * (1.0 - lam_init)
    o_diff = att.reshape(b, s, BRANCH_WIDTH) * jax.nn.silu(diff_z)

    mkv = jnp.einsum('bmd,de->bme', mem, w_mem_kv)
    mk = mkv[..., :BRANCH_WIDTH].reshape(b, MEM_LEN, MEM_HEADS, MEM_HEAD_DIM)
    mv = mkv[..., BRANCH_WIDTH:].reshape(b, MEM_LEN, MEM_HEADS, MEM_HEAD_DIM)
    mq = mq_.reshape(b, s, MEM_HEADS, MEM_HEAD_DIM)
    o_mem = memory_cross_attention(mq, mk, mv).reshape(b, s, BRANCH_WIDTH) * jax.nn.silu(mem_z)

    o = jnp.stack([o_pool, o_diff, o_mem], axis=2)
    y = jnp.einsum('bsnc,ncd->bsnd', o, w_branch)
    g = jax.nn.sigmoid((gates + b_gate).reshape(b, s, N_BRANCH, d))
    merged = jnp.sum(g * y, axis=2)
    out = jnp.einsum('bsd,de->bse', merged, w_out)
    return layer_norm(DEEPNORM_ALPHA * x + out, ln_out_g, ln_out_b)


def setup_inputs(seed: int = 0) -> dict:
    key = jax.random.key(seed)
    ks = jax.random.split(key, 20)
    f32 = jnp.float32
    d = D_MODEL
    beta = DEEPNORM_BETA
    seg_scale = (beta, 1.0, 1.0, 1.0, beta, 1.0, 1.0, 1.0, 1.0)
    col_scale = jnp.concatenate([jnp.full((wdt,), sc, f32) for wdt, sc in zip(IN_WIDTHS, seg_scale)])
    w_in = jax.random.normal(ks[2], (DEPTH, d, IN_TOTAL), f32) * (d ** -0.5) * col_scale
    kv_scale = jnp.concatenate([jnp.ones((BRANCH_WIDTH,), f32), jnp.full((BRANCH_WIDTH,), beta, f32)])
    w_mem_kv = jax.random.normal(ks[3], (DEPTH, d, 2 * BRANCH_WIDTH), f32) * (d ** -0.5) * kv_scale
    return {
        "x": jax.random.normal(ks[0], (BATCH, SEQ, d), f32),
        "mem": jax.random.normal(ks[1], (BATCH, MEM_LEN, d), f32),
        "ln_in_g": 1.0 + 0.02 * jax.random.normal(ks[4], (d,), f32),
        "ln_in_b": 0.02 * jax.random.normal(ks[5], (d,), f32),
        "w_in": w_in,
        "b_gate": 0.01 * jax.random.normal(ks[6], (DEPTH, N_BRANCH * d), f32),
        "pool_w": jax.random.normal(ks[7], (DEPTH, POOL_GROUPS, POOL_GROUP_DIM, POOL_GROUP_DIM), f32) * (POOL_GROUP_DIM ** -0.5),
        "pool_scale": 1.0 + 0.02 * jax.random.normal(ks[8], (DEPTH, BRANCH_WIDTH), f32),
        "lambda_q1": 0.1 * jax.random.normal(ks[9], (DEPTH, DIFF_QK_DIM), f32),
        "lambda_k1": 0.1 * jax.random.normal(ks[10], (DEPTH, DIFF_QK_DIM), f32),
        "lambda_q2": 0.1 * jax.random.normal(ks[11], (DEPTH, DIFF_QK_DIM), f32),
        "lambda_k2": 0.1 * jax.random.normal(ks[12], (DEPTH, DIFF_QK_DIM), f32),
        "diff_norm_g": 1.0 + 0.02 * jax.random.normal(ks[13], (DEPTH, DIFF_V_DIM), f32),
        "w_mem_kv": w_mem_kv,
        "w_branch": jax.random.normal(ks[14], (DEPTH, N_BRANCH, BRANCH_WIDTH, d), f32) * (BRANCH_WIDTH ** -0.5) * beta,
        "w_out": jax.random.normal(ks[15], (DEPTH, d, d), f32) * (d ** -0.5) * beta,
        "ln_out_g": 1.0 + 0.02 * jax.random.normal(ks[16], (DEPTH, d), f32),
        "ln_out_b": 0.02 * jax.random.normal(ks[17], (DEPTH, d), f32),
    }


def reference(x, mem, ln_in_g, ln_in_b, w_in, b_gate, pool_w, pool_scale, lambda_q1, lambda_k1,
              lambda_q2, lambda_k2, diff_norm_g, w_mem_kv, w_branch, w_out, ln_out_g, ln_out_b):
    h = layer_norm(x, ln_in_g, ln_in_b)
    for l in range(DEPTH):
        h = hybrid_layer(h, mem, l, w_in[l], b_gate[l], pool_w[l], pool_scale[l],
                         lambda_q1[l], lambda_k1[l], lambda_q2[l], lambda_k2[l], diff_norm_g[l],
                         w_mem_kv[l], w_branch[l], w_out[l], ln_out_g[l], ln_out_b[l])
    return h
```

```python
import contextlib
import numpy as np
import concourse.bass as bass
import concourse.mybir as mybir
from concourse.bass_utils import run_bass_kernel_spmd

F32 = mybir.dt.float32
BF16 = mybir.dt.bfloat16
AF = mybir.ActivationFunctionType
ALU = mybir.AluOpType
AXX = mybir.AxisListType.X

D = 1024
SEQ = 8192
BATCH = 4
NCORES = 8
CH = 512
NSLOT = 8
HALO = 128
OWN = ([0, 3, 4, 7, 8, 11, 12, 15], [1, 2, 5, 6, 9, 10, 13, 14])
ALPHA = 2.0 ** 0.25
LAM_INIT = 0.8 - 0.6 * 1.0
LN_EPS = 1e-5
RMS_EPS = 1e-5
NEG_BIG = -30000.0
ND = 24
SLOPES = [2.0 ** (-(h + 1)) for h in range(8)]
SKIP_THRESH = 144.0

RUN_SLOTS = NSLOT


class Buf:
    __slots__ = ("w", "r")

    def __init__(self):
        self.w = None
        self.r = {}


class MultiBuf:
    def __init__(self):
        self.bufs = []

    def new(self):
        b = Buf()
        self.bufs.append(b)
        return b


def _flat(bufs):
    out = []
    for b in bufs:
        if isinstance(b, MultiBuf):
            out.extend(b.bufs)
        else:
            out.append(b)
    return out


class Eng:
    def __init__(self, name, sem):
        self.name = name
        self.sem = sem
        self.cnt = 0
        self.ops = []
        self.waited = {}


class Prog:
    def __init__(self, engs, dsems):
        self.eng = engs
        self.dsems = dsems
        self.di = {q: 0 for q in dsems}

    def _deps(self, reads, writes):
        deps = []
        for b in reads:
            if b.w is not None:
                deps.append((b.w, True))
        for b in writes:
            if b.w is not None:
                deps.append((b.w, False))
            for ev in b.r.values():
                deps.append((ev, False))
        return deps

    def _emit_waits(self, E, deps, is_dma):
        for (ev, raw) in deps:
            key, sem, val = ev
            if (not is_dma) and key == E.name:
                if E.name == "pe":
                    continue
            if E.waited.get(key, 0) >= val:
                continue
            E.waited[key] = val
            E.ops.append(("wait", sem, val))

    def op(self, en, fn, reads=(), writes=()):
        reads, writes = _flat(reads), _flat(writes)
        E = self.eng[en]
        self._emit_waits(E, self._deps(reads, writes), False)
        E.cnt += 1
        ev = (en, E.sem, E.cnt)
        E.ops.append(("op", fn))
        for b in reads:
            b.r[en] = ev
        for b in writes:
            b.w = ev
            b.r = {}

    def dma(self, qn, fn, reads=(), writes=()):
        reads, writes = _flat(reads), _flat(writes)
        Q = self.eng[qn]
        pool = self.dsems[qn]
        s = pool[self.di[qn] % len(pool)]
        self.di[qn] += 1
        deps = self._deps(reads, writes)
        if s["total"] > 0:
            deps.append(((s["key"], s["sem"], s["total"]), True))
        self._emit_waits(Q, deps, True)
        s["total"] += 16
        ev = (s["key"], s["sem"], s["total"])
        Q.ops.append(("dma", fn, s["sem"]))
        for b in reads:
            b.r[s["key"]] = ev
        for b in writes:
            b.w = ev
            b.r = {}


def _run(e, E):
    for o in E.ops:
        if o[0] == "wait":
            e.wait_ge(o[1], o[2])
        elif o[0] == "op":
            o[1](e).then_inc(E.sem, 1)
        else:
            o[1](e).then_inc(o[2], 16)


class Ring:
    def __init__(self, tiles):
        self.tiles = tiles
        self.bufs = [Buf() for _ in tiles]
        self.i = 0

    def next(self):
        k = self.i % len(self.tiles)
        self.i += 1
        return self.tiles[k], self.bufs[k]


def build():
    nc = bass.Bass("TRN2", target_bir_lowering=False)
    es = contextlib.ExitStack()

    def din(name, shape, dt=F32):
        return nc.dram_tensor(name, list(shape), dt, kind="ExternalInput").ap()

    x_own = din("x_own", [NSLOT * CH, D])
    x_oth = din("x_oth", [NSLOT * CH, D])
    x_halo = din("x_halo", [NSLOT * HALO, D])
    memT = din("memT", [D, 256])
    w_in = din("w_in", [D, 7168])
    w_kv = din("w_kv", [D, 1024])
    w_br = din("w_br", [1536, D])
    w_out = din("w_out", [D, D])
    poolw = din("poolw", [128, 512])
    g_in = din("g_in", [128, D])
    b_in = din("b_in", [128, D])
    g_out = din("g_out", [128, D])
    b_out = din("b_out", [128, D])
    g_fm = din("g_fm", [128, 8])
    b_fm = din("b_fm", [128, 8])
    bgate = din("bgate", [128, 24])
    pscale = din("pscale", [128, 4])
    dng = din("dng", [128, 1])
    lam4 = din("lam4", [128, 128])
    ident_d = din("ident", [128, 128])
    blk_d = din("blk", [128, 128])
    mask_d = din("mask", [128, 128])
    tk_d = din("tk", [128, 512])
    tq_d = din("tq", [128, 2048])
    bm_d = din("bm", [NSLOT, 128, 512])
    hv_d = din("hv", [128, 8])
    invc_d = din("invc", [128, 512])
    out_own = nc.dram_tensor("out_own", [NSLOT * CH, D], F32, kind="ExternalOutput").ap()

    wbf_in = nc.dram_tensor("wbf_in", [D, 7168], BF16).ap()
    wbf_kv = nc.dram_tensor("wbf_kv", [D, 1024], BF16).ap()
    wbf_br = nc.dram_tensor("wbf_br", [1536, D], BF16).ap()
    wbf_out = nc.dram_tensor("wbf_out", [D, D], BF16).ap()
    KTd = {s: nc.dram_tensor("ktd_" + s, [4, 128, NSLOT * CH], BF16).ap() for s in ("own", "oth")}
    Vd = {s: nc.dram_tensor("vd_" + s, [NSLOT * 4, 128, 512], BF16).ap() for s in ("own", "oth")}
    KTd_B = {s: [Buf() for _ in range(NSLOT)] for s in ("own", "oth")}
    Vd_B = {s: [Buf() for _ in range(NSLOT)] for s in ("own", "oth")}
    WB = {k: Buf() for k in ("in", "kv", "br", "out")}

    def sb(name, shape, dt):
        return es.enter_context(nc.sbuf_tensor("sb_" + name, list(shape), dt))

    def sem(name):
        return es.enter_context(nc.semaphore(name))

    engs = {n: Eng(n, sem("s_" + n)) for n in ("pe", "act", "dve", "pool", "sp")}
    dsems = {q: [dict(key=("d" + q, i), sem=sem("d%s%d" % (q, i)), total=0) for i in range(n)]
             for q, n in (("sp", 16), ("pool", 12))}
    P = Prog(engs, dsems)

    ps = es.enter_context(nc.psum_tensor("ps", [128, 4096], F32))
    bankB = [Buf() for _ in range(8)]
    bank_ctr = [0]

    def nb():
        k = bank_ctr[0] % 8
        bank_ctr[0] += 1
        return k

    def bank(k, n=512, p0=0, p1=128):
        return ps[p0:p1, k * 512:k * 512 + n]

    def bank_bf(k):
        return ps[:, k * 512:(k + 1) * 512].bitcast(BF16)

    ident = sb("ident", [128, 128], BF16)
    blk = sb("blk", [128, 128], F32)
    maskb = sb("maskb", [128, 128], BF16)
    ones = sb("ones", [128, 128], BF16)
    TK = sb("TK", [128, 128], BF16)
    TQ = sb("TQ", [128, 512], BF16)
    gin = sb("gin", [128, D], F32)
    bin_ = sb("bin", [128, D], F32)
    gout = sb("gout", [128, D], F32)
    bout = sb("bout", [128, D], F32)
    bg = sb("bg", [128, 24], F32)
    gfm = sb("gfm", [128, 8], F32)
    bfm = sb("bfm", [128, 8], F32)
    psc = sb("psc", [128, 4], F32)
    gsc = sb("gsc", [128, 1], F32)
    lamt = sb("lamt", [128, 128], F32)
    lamw = sb("lamw", [128, 8], F32)
    hv = sb("hv", [128, 8], F32)
    invc = sb("invc", [128, 512], F32)
    pwb = sb("pwb", [128, 512], BF16)
    mkT = sb("mkT", [128, 4, 256], BF16)
    mvb = sb("mvb", [128, 2, 512], BF16)
    constB = MultiBuf()
    constB.new()
    mkB, mvB_ = Buf(), Buf()

    NW = 3
    wring = Ring([sb("wr%d" % i, [128, 8, 512], BF16) for i in range(NW)])
    wbring = Ring([sb("wbr%d" % i, [128, 4, 512], BF16) for i in range(3)])
    xin = Ring([sb("xin%d" % i, [128, D], F32) for i in range(2)])
    xhb = Ring([sb("xhb%d" % i, [128, D], BF16) for i in range(2)])
    stt = Ring([sb("stt%d" % i, [128, 12], F32) for i in range(2)])
    mvr = Ring([sb("mvr%d" % i, [128, 4], F32) for i in range(2)])

    xT = sb("xT", [128, 8, CH], BF16)
    xTB = Buf()
    xTh = sb("xTh", [128, 8, HALO], BF16)
    xThB = Buf()
    res = sb("res", [128, 4, D], F32)
    stage = res[:, 0:2, :].rearrange("p a b -> p (a b)")
    resB = [Buf() for _ in range(4)]
    QT = sb("QT", [128, 4, CH], BF16)
    QTB = Buf()
    mqT = sb("mqT", [128, 4, CH], BF16)
    mqB = Buf()
    sz = {k: sb("sz_" + k, [128, 4, CH], BF16) for k in ("pz", "dz", "mz")}
    szB = {k: Buf() for k in ("pz", "dz", "mz")}
    uT = sb("uT", [128, 4, HALO + CH], F32)
    uTB = [Buf() for _ in range(4)]
    pa = sb("pa", [128, HALO + CH], F32)
    pb = sb("pb", [128, HALO + CH], F32)
    paB, pbB = Buf(), Buf()
    pooled = Ring([sb("pooled%d" % i, [128, CH], BF16) for i in range(2)])
    obr = {k: sb("o_" + k, [128, 4, CH], BF16) for k in ("pool", "diff", "mem")}
    obrB = {k: Buf() for k in ("pool", "diff", "mem")}
    kst = Ring([sb("kst%d" % i, [128, CH], BF16) for i in range(2)])
    pm = Ring([sb("pm%d" % i, [128, 2, CH], BF16) for i in range(2)])
    ftmp = Ring([sb("ftmp%d" % i, [128, CH], F32) for i in range(4)])
    gsig = Ring([sb("gsig%d" % i, [128, CH], BF16) for i in range(3)])
    merged = sb("merged", [128, 8, CH], BF16)
    mergedB = Buf()
    rbuf = Ring([sb("rbuf%d" % i, [128, D], F32) for i in range(3)])
    kbufs = Ring([sb("kbuf%d" % i, [128, CH], BF16) for i in range(2)])
    vbufs = Ring([sb("vbuf%d" % i, [128, 4, 128], BF16) for i in range(2)])
    PTall = sb("ptall", [128, 3, 2048], BF16)
    PTs = [PTall[:, i, :].rearrange("p (a b) -> p a b", a=2) for i in range(3)]
    PTB = [[Buf(), Buf()] for _ in range(3)]
    xTo = PTall[:, 0:2, :].rearrange("p a b -> p (a b)").rearrange("p (kc t) -> p kc t", kc=8)
    xToB = MultiBuf()
    xToB.new()
    xToB.bufs.extend([PTB[0][0], PTB[0][1], PTB[1][0], PTB[1][1]])
    bmr = Ring([sb("bmr%d" % i, [128, 512], F32) for i in range(2)])

    def load_const(dst, src, cast_via=None):
        cb = constB.new()
        if cast_via is None:
            P.dma("sp", lambda e, d=dst, s=src: e.dma_start(out=d, in_=s), writes=[cb])
        else:
            sb_ = Buf()
            P.dma("sp", lambda e, d=cast_via, s=src: e.dma_start(out=d, in_=s), writes=[sb_])
            P.op("dve", lambda e, d=dst, s=cast_via: e.tensor_copy(out=d, in_=s),
                 reads=[sb_], writes=[cb, resB[0], resB[1]])

    WBg = {}

    def cast_w(dst, src, key, r0, nrows, c0, ncols):
        b = Buf()
        for c in range(c0, c0 + ncols, 512):
            WBg[(key, r0, c)] = b
        P.dma("pool", lambda e, d=dst[r0:r0 + nrows, c0:c0 + ncols], s=src[r0:r0 + nrows, c0:c0 + ncols]:
              e.dma_start(out=d, in_=s), writes=[b])

    cast_w(wbf_in, w_in, "in", 0, D, 1536, 1024)
    cast_w(wbf_in, w_in, "in", 0, D, 0, 1536)
    cast_w(wbf_in, w_in, "in", 0, D, 2560, 1536)
    cast_w(wbf_kv, w_kv, "kv", 0, D, 0, 1024)
    cast_w(wbf_in, w_in, "in", 0, D, 4096, 3072)
    for n in range(3):
        cast_w(wbf_br, w_br, "br", n * 512, 512, 0, 1024)
    cast_w(wbf_out, w_out, "out", 0, D, 0, 1024)

    def cast_batch1():
        pass

    def cast_batch2():
        pass

    load_const(ident[:, :], ident_d, stage[:, 0:128])
    load_const(maskb[:, :], mask_d, stage[:, 128:256])
    load_const(TK[:, :], tk_d[:, 0:128], stage[:, 256:384])
    load_const(pwb[:, :], poolw, stage[:, 768:1280])
    load_const(blk[:, :], blk_d)
    load_const(gin[:, :], g_in)
    load_const(bin_[:, :], b_in)
    load_const(gout[:, :], g_out)
    load_const(bout[:, :], b_out)
    load_const(bg[:, :], bgate)
    load_const(gfm[:, :], g_fm)
    load_const(bfm[:, :], b_fm)
    load_const(psc[:, :], pscale)
    load_const(gsc[:, :], dng)
    load_const(lamt[:, :], lam4)
    load_const(hv[:, :], hv_d)
    load_const(invc[:, :], invc_d)
    load_const(TQ[:, :], tq_d[:, 0:512], stage[:, 1280:1792])
    P.op("dve", lambda e: e.memset(ones[:, :], 1.0), writes=[constB])
    P.op("dve", lambda e: e.tensor_scalar(out=gin[:, :], in0=gin[:, :], scalar1=ALPHA, scalar2=None, op0=ALU.mult),
         reads=[constB], writes=[constB])
    P.op("dve", lambda e: e.tensor_scalar(out=bin_[:, :], in0=bin_[:, :], scalar1=ALPHA, scalar2=None, op0=ALU.mult),
         reads=[constB], writes=[constB])
    P.op("dve", lambda e: e.tensor_scalar(out=gsc[:, :], in0=gsc[:, :], scalar1=(1.0 - LAM_INIT) * 8.0, scalar2=None,
                                          op0=ALU.mult), reads=[constB], writes=[constB])
    P.op("dve", lambda e: e.tensor_tensor(out=lamt[:, 0:32], in0=lamt[:, 0:32], in1=lamt[:, 32:64], op=ALU.mult),
         reads=[constB], writes=[constB])
    P.op("dve", lambda e: e.tensor_tensor(out=lamt[:, 64:96], in0=lamt[:, 64:96], in1=lamt[:, 96:128], op=ALU.mult),
         reads=[constB], writes=[constB])
    P.op("dve", lambda e: e.reduce_sum(out=lamw[:, 0:1], in_=lamt[:, 0:32], axis=AXX), reads=[constB], writes=[constB])
    P.op("dve", lambda e: e.reduce_sum(out=lamw[:, 1:2], in_=lamt[:, 64:96], axis=AXX), reads=[constB], writes=[constB])
    P.op("act", lambda e: e.activation(out=lamw[:, 2:4], in_=lamw[:, 0:2], func=AF.Exp), reads=[constB], writes=[constB])
    P.op("dve", lambda e: e.tensor_tensor(out=lamw[:, 4:5], in0=lamw[:, 3:4], in1=lamw[:, 2:3], op=ALU.subtract),
         reads=[constB], writes=[constB])
    P.op("dve", lambda e: e.tensor_scalar(out=lamw[:, 5:6], in0=lamw[:, 4:5], scalar1=-LAM_INIT, scalar2=None,
                                          op0=ALU.add), reads=[constB], writes=[constB])
    neglam = lamw[:, 5:6]

    def wsrc(w, r0, nkc, c0):
        return w[r0:r0 + nkc * 128, c0:c0 + 512].rearrange("(kc p) c -> p kc c", p=128)

    def load_w(w, key, r0, nkc, c0, dst=None, dstB=None):
        if dst is None:
            dst, dstB = wring.next()
        P.dma("sp", lambda e, d=dst[:, 0:nkc, :], s=wsrc(w, r0, nkc, c0): e.dma_start(out=d, in_=s),
              reads=[WBg[(key, r0, c0)]], writes=[dstB])
        return dst, dstB


    def mm_group(out_ap, pairs, bufs_r, bufB, tp=None):
        def f(e, out_ap=out_ap, pairs=pairs):
            n = len(pairs)
            ins = None
            for i, (l, r) in enumerate(pairs):
                ins = e.matmul(out_ap, lhsT=l, rhs=r, start=(i == 0), stop=(i == n - 1))
            return ins
        P.op("pe", f, reads=bufs_r, writes=[bufB])

    def mem_kv():
        halves = []
        for hh in range(2):
            stg, stgB = rbuf.next()
            P.dma("sp", lambda e, stg=stg, hh=hh: e.dma_start(
                out=stg[:, :].rearrange("p (kc m) -> p kc m", kc=4),
                in_=memT[hh * 512:(hh + 1) * 512, :].rearrange("(kc p) m -> p kc m", p=128)), writes=[stgB])
            halves.append((stg, stgB))
        for hh, (stg, stgB) in enumerate(halves):
            P.op("dve", lambda e, stg=stg, hh=hh: e.tensor_copy(
                out=xTo[:, 4 * hh:4 * hh + 4, 0:256], in_=stg[:, :].rearrange("p (kc m) -> p kc m", kc=4)),
                reads=[stgB], writes=[xToB])
        wmk, wmkB = load_w(wbf_kv, "kv", 0, 8, 0)
        wmv, wmvB = load_w(wbf_kv, "kv", 0, 8, 512)
        for hd in range(4):
            k = nb()
            mm_group(bank(k, 256), [(wmk[:, kc, hd * 128:(hd + 1) * 128], xTo[:, kc, 0:256]) for kc in range(8)],
                     [wmkB, xToB], bankB[k])
            P.op("dve", lambda e, k=k, hd=hd: e.tensor_copy(out=mkT[:, hd, :], in_=bank(k, 256)),
                 reads=[bankB[k]], writes=[mkB])
        for mt in range(2):
            k = nb()
            mm_group(bank(k), [(xTo[:, kc, mt * 128:(mt + 1) * 128], wmv[:, kc, :]) for kc in range(8)],
                     [wmvB, xToB], bankB[k])
            P.op("dve", lambda e, k=k, mt=mt: e.tensor_copy(out=mvb[:, mt, :], in_=bank(k)),
                 reads=[bankB[k]], writes=[mvB_])

    def skew(stages):
        prev = None
        for pa_, pb_ in stages:
            pa_()
            if prev is not None:
                prev()
            prev = pb_
        if prev is not None:
            prev()

    def ln_tile(src_rows, nrows, dstT, dstTB, col0, keep=None):
        xt, xB = xin.next()
        st, stB = stt.next()
        mv, mvB2 = mvr.next()
        k = nb()
        bt = bank_bf(k)
        hb, hbB = xhb.next()
        btv = bt.rearrange("p (kc t) -> p kc t", kc=8)[:, :, 0:nrows]

        def tr(e):
            ins = None
            for kc in range(8):
                ins = e.transpose(out=bt[:, kc * 128:kc * 128 + nrows], in_=hb[0:nrows, kc * 128:(kc + 1) * 128],
                                  identity=ident[0:nrows, 0:nrows])
            return ins

        def stage_a():
            P.dma("sp", lambda e: e.dma_start(out=xt[0:nrows, :], in_=src_rows), writes=[xB])
            P.op("dve", lambda e: e.bn_stats(out=st[0:nrows, 0:6], in_=xt[0:nrows, 0:512]), reads=[xB], writes=[stB])
            P.op("dve", lambda e: e.bn_stats(out=st[0:nrows, 6:12], in_=xt[0:nrows, 512:1024]), reads=[xB], writes=[stB])
            P.op("dve", lambda e: e.bn_aggr(out=mv[0:nrows, 0:2], in_=st[0:nrows, 0:12]), reads=[stB], writes=[mvB2])
            P.op("dve", lambda e: e.tensor_scalar(out=mv[0:nrows, 2:3], in0=mv[0:nrows, 1:2], scalar1=LN_EPS,
                                                  scalar2=None, op0=ALU.add), reads=[mvB2], writes=[mvB2])
            P.op("act", lambda e: e.activation(out=mv[0:nrows, 2:3], in_=mv[0:nrows, 2:3], func=AF.Sqrt),
                 reads=[mvB2], writes=[mvB2])
            P.op("dve", lambda e: e.reciprocal(out=mv[0:nrows, 2:3], in_=mv[0:nrows, 2:3]), reads=[mvB2], writes=[mvB2])
            P.op("dve", lambda e: e.scalar_tensor_tensor(out=mv[0:nrows, 3:4], in0=mv[0:nrows, 0:1], scalar=-1.0,
                                                         in1=mv[0:nrows, 2:3], op0=ALU.mult, op1=ALU.mult),
                 reads=[mvB2], writes=[mvB2])
            if keep is None:
                P.op("act", lambda e: e.activation(out=hb[0:nrows, :], in_=xt[0:nrows, :], func=AF.Identity,
                                                   bias=mv[0:nrows, 3:4], scale=mv[0:nrows, 2:3]),
                     reads=[xB, mvB2], writes=[hbB])
            else:
                rt, rB = keep
                P.op("act", lambda e: e.activation(out=rt, in_=xt[0:nrows, :], func=AF.Identity, bias=mv[0:nrows, 3:4],
                                                   scale=mv[0:nrows, 2:3]), reads=[xB, mvB2], writes=[rB])

        def stage_b():
            if keep is None:
                P.op("pe", tr, reads=[hbB, constB], writes=[bankB[k]])
                tmp, tmpB = rbuf.next()
                tv = tmp[:, :].rearrange("p (kc t) -> p kc t", kc=8)[:, :, 0:nrows]
                P.op("dve", lambda e: e.tensor_tensor(out=tv, in0=btv,
                                                      in1=gfm[:, :].unsqueeze(2).to_broadcast([128, 8, nrows]),
                                                      op=ALU.mult), reads=[bankB[k], constB], writes=[tmpB])
                P.op("dve", lambda e: e.tensor_tensor(out=dstT[:, :, col0:col0 + nrows], in0=tv,
                                                      in1=bfm[:, :].unsqueeze(2).to_broadcast([128, 8, nrows]),
                                                      op=ALU.add), reads=[tmpB, constB], writes=[dstTB])
            else:
                rt, rB = keep
                P.op("dve", lambda e: e.tensor_tensor(out=rt, in0=rt, in1=gin[0:nrows, :], op=ALU.mult),
                     reads=[rB, constB], writes=[rB])
                P.op("pool", lambda e: e.tensor_tensor(out=rt, in0=rt, in1=bin_[0:nrows, :], op=ALU.add),
                     reads=[rB, constB], writes=[rB])
                P.op("dve", lambda e: e.tensor_scalar(out=hb[0:nrows, :], in0=rt, scalar1=1.0 / ALPHA, scalar2=None,
                                                      op0=ALU.mult), reads=[rB], writes=[hbB])
                P.op("pe", tr, reads=[hbB, constB], writes=[bankB[k]])
                P.op("act", lambda e: e.activation(out=dstT[:, :, col0:col0 + nrows], in_=btv, func=AF.Copy),
                     reads=[bankB[k]], writes=[dstTB])
        return stage_a, stage_b

    def kv_project(src_xT, src_B, store, j, part=None):
        if part in (None, "k"):
            wk, wkB = load_w(wbf_in, "in", 0, 8, 3 * 512)
        if part in (None, "v"):
            wv, wvB = load_w(wbf_in, "in", 0, 8, 4 * 512)
        for hp in (range(4) if part in (None, "k") else ()):
            k = nb()
            mm_group(bank(k), [(wk[:, kc, hp * 128:(hp + 1) * 128], src_xT[:, kc, :]) for kc in range(8)],
                     [wkB, src_B], bankB[k])
            ks, ksB = kst.next()
            P.op("act", lambda e, k=k, ks=ks: e.activation(out=ks[:, :], in_=bank(k), func=AF.Copy),
                 reads=[bankB[k]], writes=[ksB])
            P.dma("pool", lambda e, ks=ks, hp=hp: e.dma_start(out=KTd[store][hp, :, j * CH:(j + 1) * CH], in_=ks[:, :]),
                  reads=[ksB], writes=[KTd_B[store][j]])
        for tt in (range(4) if part in (None, "v") else ()):
            k = nb()
            mm_group(bank(k), [(src_xT[:, kc, tt * 128:(tt + 1) * 128], wv[:, kc, :]) for kc in range(8)],
                     [wvB, src_B], bankB[k])
            ks, ksB = kst.next()
            P.op("dve", lambda e, k=k, ks=ks: e.tensor_copy(out=ks[:, :], in_=bank(k)),
                 reads=[bankB[k]], writes=[ksB])
            P.dma("pool", lambda e, ks=ks, tt=tt: e.dma_start(out=Vd[store][j * 4 + tt, :, :], in_=ks[:, :]),
                  reads=[ksB], writes=[Vd_B[store][j]])

    def fm_group(wt, wtB, ct, src_xT, src_B, n=CH):
        k = nb()
        mm_group(bank(k, n), [(wt[:, kc, ct * 128:(ct + 1) * 128], src_xT[:, kc, 0:n]) for kc in range(8)],
                 [wtB, src_B], bankB[k])
        return k

    def stage_A(jj):
        skew([ln_tile(x_oth[jj * CH + tt * 128:jj * CH + (tt + 1) * 128, :], 128, xTo, xToB, tt * 128)
              for tt in range(4)])
        kv_project(xTo, xToB, "oth", jj)

    for j in range(RUN_SLOTS):
        bmt, bmB = bmr.next()
        P.dma("sp", lambda e, bmt=bmt, j=j: e.dma_start(out=bmt[:, :], in_=bm_d[j, :, :]), writes=[bmB])

        if j == 0:
            stage_A(0)
            cast_batch1()

        skew([ln_tile(x_own[j * CH + tt * 128:j * CH + (tt + 1) * 128, :], 128, xT, xTB, tt * 128,
                      keep=(res[:, tt, :], resB[tt])) for tt in range(4)]
             + [ln_tile(x_halo[j * HALO:(j + 1) * HALO, :], HALO, xTh, xThB, 0)])
        if j == 0:
            cast_batch2()
        kv_project(xT, xTB, "own", j)

        wt, wtB = load_w(wbf_in, "in", 0, 8, 0)
        for g in range(4):
            k = fm_group(wt, wtB, g, xT, xTB)
            P.op("dve", lambda e, k=k, g=g: e.tensor_copy(out=uT[:, g, HALO:HALO + CH], in_=bank(k)),
                 reads=[bankB[k]], writes=[uTB[g]])
            k = fm_group(wt, wtB, g, xTh, xThB, n=HALO)
            P.op("dve", lambda e, k=k, g=g, j=j: e.tensor_scalar(out=uT[:, g, 0:HALO], in0=bank(k, HALO),
                                                                 scalar1=hv[:, j:j + 1], scalar2=None, op0=ALU.mult),
                 reads=[bankB[k], constB], writes=[uTB[g]])
        def proj_silu(grp, kind):
            wt, wtB = load_w(wbf_in, "in", 0, 8, grp * 512)
            for ct in range(4):
                k = fm_group(wt, wtB, ct, xT, xTB)
                P.op("act", lambda e, k=k, ct=ct, kind=kind: e.activation(out=sz[kind][:, ct, :], in_=bank(k),
                                                                          func=AF.Silu),
                     reads=[bankB[k]], writes=[szB[kind]])
        proj_silu(1, "pz")
        W = HALO + CH
        for g, w in enumerate((2, 4, 8, 16)):
            U = uT[:, g, :]
            P.op("pool", lambda e, U=U: e.tensor_tensor(out=pa[:, 1:W], in0=U[:, 1:W], in1=U[:, 0:W - 1], op=ALU.add),
                 reads=[uTB[g]], writes=[paB])
            cur, curB = pa, paB
            if w >= 4:
                P.op("pool", lambda e: e.tensor_tensor(out=pb[:, 3:W], in0=pa[:, 3:W], in1=pa[:, 1:W - 2], op=ALU.add),
                     reads=[paB], writes=[pbB])
                cur, curB = pb, pbB
            if w >= 8:
                P.op("pool", lambda e: e.tensor_tensor(out=pa[:, 7:W], in0=pb[:, 7:W], in1=pb[:, 3:W - 4], op=ALU.add),
                     reads=[pbB], writes=[paB])
                cur, curB = pa, paB
            if w >= 16:
                P.op("pool", lambda e: e.tensor_tensor(out=pb[:, 15:W], in0=pa[:, 15:W], in1=pa[:, 7:W - 8], op=ALU.add),
                     reads=[paB], writes=[pbB])
                cur, curB = pb, pbB
            pl, plB = pooled.next()
            P.op("dve", lambda e, cur=cur, U=U, w=w, pl=pl: e.scalar_tensor_tensor(
                out=pl[:, :], in0=cur[:, HALO:W], scalar=1.0 / w, in1=U[:, HALO:W], op0=ALU.mult, op1=ALU.subtract),
                reads=[curB, uTB[g]], writes=[plB])
            ic = invc[:, (j * 4 + g) * 16:(j * 4 + g + 1) * 16]
            P.op("pool", lambda e, cur=cur, ic=ic: e.tensor_tensor(out=cur[:, 0:16], in0=cur[:, HALO:HALO + 16], in1=ic, op=ALU.mult),
                 reads=[curB, constB, plB], writes=[curB])
            P.op("pool", lambda e, cur=cur, U=U, pl=pl: e.tensor_tensor(out=pl[:, 0:16], in0=cur[:, 0:16], in1=U[:, HALO:HALO + 16],
                                                                         op=ALU.subtract),
                 reads=[curB, uTB[g]], writes=[plB])
            k = nb()
            mm_group(bank(k), [(pwb[:, g * 128:(g + 1) * 128], pl[:, :])], [plB, constB], bankB[k])
            P.op("dve", lambda e, k=k, g=g: e.scalar_tensor_tensor(
                out=obr["pool"][:, g, :], in0=bank(k), scalar=psc[:, g:g + 1], in1=sz["pz"][:, g, :],
                op0=ALU.mult, op1=ALU.mult), reads=[bankB[k], szB["pz"], constB], writes=[obrB["pool"]])

        proj_silu(5, "dz")
        proj_silu(7, "mz")
        wt, wtB = load_w(wbf_in, "in", 0, 8, 2 * 512)
        for ct in range(4):
            k = fm_group(wt, wtB, ct, xT, xTB)
            P.op("dve", lambda e, k=k, ct=ct: e.tensor_scalar(out=QT[:, ct, :], in0=bank(k), scalar1=32.0 ** -0.5,
                                                              scalar2=None, op0=ALU.mult),
                 reads=[bankB[k]], writes=[QTB])
        wt, wtB = load_w(wbf_in, "in", 0, 8, 6 * 512)
        for ct in range(4):
            k = fm_group(wt, wtB, ct, xT, xTB)
            P.op("dve", lambda e, k=k, ct=ct: e.tensor_scalar(out=mqT[:, ct, :], in0=bank(k), scalar1=128.0 ** -0.5,
                                                              scalar2=None, op0=ALU.mult),
                 reads=[bankB[k]], writes=[mqB])

        if j == 0:
            mem_kv()
        for hd in range(4):
            kk = (nb() // 2) * 2
            bank_ctr[0] = kk + 2
            pmt, pmB = pm.next()
            for mt in range(2):
                mm_group(bank(kk + mt), [(mkT[:, hd, mt * 128:(mt + 1) * 128], mqT[:, hd, :])], [mkB, mqB],
                         bankB[kk + mt])
            P.op("act", lambda e, kk=kk, pmt=pmt: e.activation(
                out=pmt[:, :, :].rearrange("p a b -> p (a b)"), in_=ps[:, kk * 512:(kk + 2) * 512], func=AF.Exp),
                reads=[bankB[kk], bankB[kk + 1]], writes=[pmB])
            ko = nb()
            mm_group(bank(ko), [(mvb[:, mt, hd * 128:(hd + 1) * 128], pmt[:, mt, :]) for mt in range(2)],
                     [mvB_, pmB], bankB[ko])
            ksum = nb()
            mm_group(bank(ksum), [(ones[:, :], pmt[:, mt, :]) for mt in range(2)], [constB, pmB], bankB[ksum])
            f1, f1B = ftmp.next()
            f2, f2B = ftmp.next()
            P.op("dve", lambda e, ksum=ksum, f1=f1: e.reciprocal(out=f1[:, :], in_=bank(ksum)),
                 reads=[bankB[ksum]], writes=[f1B])
            P.op("dve", lambda e, ko=ko, f1=f1, f2=f2: e.tensor_tensor(out=f2[:, :], in0=bank(ko), in1=f1[:, :], op=ALU.mult),
                 reads=[bankB[ko], f1B], writes=[f2B])
            P.op("dve", lambda e, f2=f2, hd=hd: e.tensor_tensor(out=obr["mem"][:, hd, :], in0=f2[:, :],
                                                                in1=sz["mz"][:, hd, :], op=ALU.mult),
                 reads=[f2B, szB["mz"]], writes=[obrB["mem"]])

        entries = [("own", i, False) for i in range(j)] + [("oth", i, False) for i in range(j + 1)] + [("own", j, True)]
        step = [0]

        def live(st_, si, kt, h, diag):
            if diag:
                return True
            for r in range(2):
                cq = OWN[r][j]
                kc = (OWN[r] if st_ == "own" else OWN[1 - r])[si]
                if kc > cq:
                    continue
                if SLOPES[h] * (512 * (cq - kc) - 128 * kt - 127) <= SKIP_THRESH:
                    return True
            return False

        deferred = [None, 0]
        for hp in range(4):
            first = [True, True]
            pending = [None]
            for ei, (st_, si, diag) in enumerate(entries):
                lv = {(kt, hl): live(st_, si, kt, 2 * hp + hl, diag) for kt in range(4) for hl in range(2)}
                if not any(lv.values()):
                    continue
                kb, kB = kbufs.next()
                vb, vB = vbufs.next()
                P.dma("sp", lambda e, kb=kb, st_=st_, si=si, hp=hp: e.dma_start(
                    out=kb[:, :], in_=KTd[st_][hp, :, si * CH:(si + 1) * CH]), reads=[KTd_B[st_][si]], writes=[kB])
                P.dma("sp", lambda e, vb=vb, st_=st_, si=si, hp=hp: e.dma_start(
                    out=vb[:, :, :], in_=Vd[st_][si * 4:(si + 1) * 4, :, hp * 128:(hp + 1) * 128].rearrange("kt p c -> p kt c")),
                    reads=[Vd_B[st_][si]], writes=[vB])
                kts = (3, 2, 1, 0) if diag else (0, 1, 2, 3)
                for kt in kts:
                    lo = kt * 128 if diag else 0
                    last = diag and kt == 0
                    hls = [hl for hl in range(2) if lv[(kt, hl)]]
                    if not hls:
                        continue
                    pi = step[0] % 3
                    step[0] += 1
                    pt = PTs[pi]
                    for hl in hls:
                        h = 2 * hp + hl
                        sB = [bankB[2 * hl], bankB[2 * hl + 1]]

                        def fsc(e, kb=kb, kt=kt, hl=hl, hp=hp, lo=lo):
                            ins = None
                            for c in range(2):
                                g = 2 * hl + c
                                o = ps[:, g * 512 + lo:(g + 1) * 512]
                                ins = e.matmul(o, lhsT=kb[32 * g:32 * g + 32, kt * 128:(kt + 1) * 128],
                                               rhs=QT[32 * g:32 * g + 32, hp, lo:CH], start=True, stop=(hp >= 1),
                                               tile_position=(32 * g, 0))
                                if hp == 0:
                                    ins = e.matmul(o, lhsT=TK[32 * g:32 * g + 3, hp * 128:(hp + 1) * 128],
                                                   rhs=TQ[32 * g:32 * g + 3, hp * 512 + lo:(hp + 1) * 512],
                                                   start=False, stop=True, tile_position=(32 * g, 0))
                            return ins
                        P.op("pe", fsc, reads=[kB, QTB, constB], writes=sB)
                        bidx = (ei * 4 + kt) * 8 + h
                        P.op("act", lambda e, pt=pt, hl=hl, lo=lo, bidx=bidx, bmt=bmt: e.activation(
                            out=pt[:, hl, :].rearrange("p (c q) -> p c q", c=2)[:, :, lo:CH],
                            in_=ps[:, 2 * hl * 512:(2 * hl + 2) * 512].rearrange("p (c q) -> p c q", c=2)[:, :, lo:CH],
                            func=AF.Exp, bias=bmt[:, bidx:bidx + 1], scale=1.0),
                            reads=sB + [bmB], writes=[PTB[pi][hl]])
                        if diag:
                            P.op("dve", lambda e, pt=pt, hl=hl, lo=lo: e.tensor_tensor(
                                out=pt[:, hl, :].rearrange("p (c q) -> p c q", c=2)[:, :, lo:lo + 128],
                                in0=pt[:, hl, :].rearrange("p (c q) -> p c q", c=2)[:, :, lo:lo + 128],
                                in1=maskb[:, :].unsqueeze(1).to_broadcast([128, 2, 128]), op=ALU.mult),
                                reads=[PTB[pi][hl], constB], writes=[PTB[pi][hl]])

                    def fpv(e, vb=vb, kt=kt, pt=pt, lo=lo, fst=tuple(first), last=last, hls=tuple(hls)):
                        ins = None
                        for c in range(2):
                            for hl in hls:
                                rhs = pt[:, hl, c * 512 + lo:(c + 1) * 512]
                                e.matmul(ps[64 * hl:64 * hl + 64, (4 + c) * 512 + lo:(5 + c) * 512],
                                         lhsT=vb[:, kt, 64 * hl:64 * hl + 64], rhs=rhs, start=fst[hl], stop=last,
                                         tile_position=(0, 64 * hl))
                                ins = e.matmul(ps[64 * hl:64 * hl + 64, (6 + c) * 512 + lo:(7 + c) * 512],
                                               lhsT=ones[:, 0:64], rhs=rhs, start=fst[hl], stop=last,
                                               tile_position=(0, 64 * hl))
                        return ins
                    pv_args = (fpv, [vB, constB] + [PTB[pi][hl] for hl in hls], [bankB[4], bankB[5], bankB[6], bankB[7]])
                    for hl in hls:
                        first[hl] = False
                    if pending[0] is not None:
                        P.op("pe", pending[0][0], reads=pending[0][1], writes=pending[0][2])
                    pending[0] = pv_args
                    if deferred[0] is not None:
                        deferred[1] -= 1
                        if deferred[1] <= 0:
                            deferred[0]()
                            deferred[0] = None
            P.op("pe", pending[0][0], reads=pending[0][1], writes=pending[0][2])
            if deferred[0] is not None:
                deferred[0]()
                deferred[0] = None
            r0, r0B = ftmp.next()
            r1, r1B = ftmp.next()
            a0, a0B = ftmp.next()
            a1, a1B = ftmp.next()
            P.op("dve", lambda e, a0=a0: e.tensor_copy(out=a0[:, :], in_=bank(4)), reads=[bankB[4]], writes=[a0B])
            P.op("dve", lambda e, a1=a1: e.tensor_copy(out=a1[:, :], in_=bank(5)), reads=[bankB[5]], writes=[a1B])
            P.op("dve", lambda e, r0=r0: e.tensor_copy(out=r0[:, :], in_=bank(6)), reads=[bankB[6]], writes=[r0B])
            P.op("dve", lambda e, r1=r1: e.tensor_copy(out=r1[:, :], in_=bank(7)), reads=[bankB[7]], writes=[r1B])
            P.op("dve", lambda e, r0=r0: e.reciprocal(out=r0[:, :], in_=r0[:, :]), reads=[r0B], writes=[r0B])
            P.op("dve", lambda e, r1=r1: e.reciprocal(out=r1[:, :], in_=r1[:, :]), reads=[r1B], writes=[r1B])
            P.op("dve", lambda e, r0=r0, a0=a0: e.tensor_tensor(out=a0[:, :], in0=a0[:, :], in1=r0[:, :], op=ALU.mult),
                 reads=[a0B, r0B], writes=[a0B])
            P.op("dve", lambda e, r1=r1, a1=a1: e.tensor_tensor(out=a1[:, :], in0=a1[:, :], in1=r1[:, :], op=ALU.mult),
                 reads=[a1B, r1B], writes=[a1B])
            P.op("dve", lambda e, a0=a0, a1=a1: e.scalar_tensor_tensor(out=a0[:, :], in0=a1[:, :], scalar=neglam,
                                                                       in1=a0[:, :], op0=ALU.mult, op1=ALU.add),
                 reads=[a0B, a1B, constB], writes=[a0B])
            P.op("dve", lambda e, a0=a0, a1=a1: e.tensor_tensor(out=a1[:, :], in0=a0[:, :], in1=a0[:, :], op=ALU.mult),
                 reads=[a0B], writes=[a1B])

            def epi2(r0=r0, r0B=r0B, a0=a0, a0B=a0B, a1=a1, a1B=a1B, hp=hp):
                mm_group(bank(0), [(blk[:, :], a1[:, :])], [constB, a1B], bankB[0])
                P.op("dve", lambda e: e.tensor_scalar(out=r0[:, :], in0=bank(0), scalar1=64.0 * RMS_EPS, scalar2=None,
                                                      op0=ALU.add), reads=[bankB[0]], writes=[r0B])
                P.op("act", lambda e: e.activation(out=r0[:, :], in_=r0[:, :], func=AF.Sqrt), reads=[r0B], writes=[r0B])
                P.op("dve", lambda e: e.reciprocal(out=r0[:, :], in_=r0[:, :]), reads=[r0B], writes=[r0B])
                P.op("dve", lambda e: e.tensor_tensor(out=a0[:, :], in0=a0[:, :], in1=r0[:, :], op=ALU.mult),
                     reads=[a0B, r0B], writes=[a0B])
                P.op("dve", lambda e: e.scalar_tensor_tensor(
                    out=obr["diff"][:, hp, :], in0=a0[:, :], scalar=gsc[:, 0:1], in1=sz["dz"][:, hp, :],
                    op0=ALU.mult, op1=ALU.mult), reads=[a0B, szB["dz"], constB], writes=[obrB["diff"]])
            if hp < 3:
                deferred[0], deferred[1] = epi2, 6
            else:
                epi2()

        sched = {}
        if j + 1 < RUN_SLOTS:
            jn = j + 1
            tl = [ln_tile(x_oth[jn * CH + tt * 128:jn * CH + (tt + 1) * 128, :], 128, xTo, xToB, tt * 128)
                  for tt in range(4)]
            sched = {0: [tl[0][0]], 1: [tl[1][0]], 2: [tl[0][1], tl[2][0]], 3: [tl[1][1], tl[3][0]],
                     4: [tl[2][1]], 5: [tl[3][1]]}
        for hf in range(2):
            wg = []
            for n in range(3):
                wg.append(load_w(wbf_in, "in", 0, 8, (8 + 2 * n + hf) * 512))
            for dl in range(4):
                dt_ = 4 * hf + dl
                for piece in sched.get(dt_, ()):
                    piece()
                gs = []
                for n in range(3):
                    k = fm_group(wg[n][0], wg[n][1], dl, xT, xTB)
                    gt, gB = gsig.next()
                    P.op("act", lambda e, k=k, gt=gt, n=n, dt_=dt_: e.activation(
                        out=gt[:, :], in_=bank(k), func=AF.Sigmoid, bias=bg[:, n * 8 + dt_:n * 8 + dt_ + 1], scale=1.0),
                        reads=[bankB[k], constB], writes=[gB])
                    gs.append((gt, gB))
                if dl == 0:
                    wb = []
                    for n in range(3):
                        wbt, wbB = wbring.next()
                        wb.append(load_w(wbf_br, "br", n * 512, 4, hf * 512, wbt, wbB))
                ys = []
                for n, key in enumerate(("pool", "diff", "mem")):
                    k = nb()
                    mm_group(bank(k), [(wb[n][0][:, cc, dl * 128:(dl + 1) * 128], obr[key][:, cc, :]) for cc in range(4)],
                             [wb[n][1], obrB[key]], bankB[k])
                    ys.append(k)
                m, mB = ftmp.next()
                t, tB = ftmp.next()
                P.op("dve", lambda e, m=m, k=ys[0], g=gs[0][0]: e.tensor_tensor(out=m[:, :], in0=bank(k), in1=g[:, :], op=ALU.mult),
                     reads=[bankB[ys[0]], gs[0][1]], writes=[mB])
                P.op("dve", lambda e, t=t, k=ys[1], g=gs[1][0]: e.tensor_tensor(out=t[:, :], in0=bank(k), in1=g[:, :], op=ALU.mult),
                     reads=[bankB[ys[1]], gs[1][1]], writes=[tB])
                P.op("pool", lambda e, m=m, t=t: e.tensor_tensor(out=m[:, :], in0=m[:, :], in1=t[:, :], op=ALU.add),
                     reads=[mB, tB], writes=[mB])
                t2, t2B = ftmp.next()
                P.op("dve", lambda e, t2=t2, k=ys[2], g=gs[2][0]: e.tensor_tensor(out=t2[:, :], in0=bank(k), in1=g[:, :], op=ALU.mult),
                     reads=[bankB[ys[2]], gs[2][1]], writes=[t2B])
                P.op("pool", lambda e, m=m, t2=t2, dt_=dt_: e.tensor_tensor(out=merged[:, dt_, :], in0=m[:, :], in1=t2[:, :], op=ALU.add),
                     reads=[mB, t2B], writes=[mergedB])
        if j + 1 < RUN_SLOTS:
            kv_project(xTo, xToB, "oth", j + 1)
        wo = [load_w(wbf_out, "out", 0, 8, hf2 * 512) for hf2 in range(2)]
        def final_tile(tt, j=j):
            rb, rbB = rbuf.next()
            st, stB = stt.next()
            mv, mvB2 = mvr.next()

            def fa():
                for hf2 in range(2):
                    k = nb()
                    mm_group(bank(k), [(merged[:, kc, tt * 128:(tt + 1) * 128], wo[hf2][0][:, kc, :]) for kc in range(8)],
                             [mergedB, wo[hf2][1]], bankB[k])
                    P.op("dve", lambda e, k=k, hf2=hf2: e.tensor_tensor(
                        out=rb[:, hf2 * 512:(hf2 + 1) * 512], in0=bank(k), in1=res[:, tt, hf2 * 512:(hf2 + 1) * 512],
                        op=ALU.add), reads=[bankB[k], resB[tt]], writes=[rbB])
                P.op("dve", lambda e: e.bn_stats(out=st[:, 0:6], in_=rb[:, 0:512]), reads=[rbB], writes=[stB])
                P.op("dve", lambda e: e.bn_stats(out=st[:, 6:12], in_=rb[:, 512:1024]), reads=[rbB], writes=[stB])
                P.op("dve", lambda e: e.bn_aggr(out=mv[:, 0:2], in_=st[:, 0:12]), reads=[stB], writes=[mvB2])
                P.op("dve", lambda e: e.tensor_scalar(out=mv[:, 2:3], in0=mv[:, 1:2], scalar1=LN_EPS, scalar2=None,
                                                      op0=ALU.add), reads=[mvB2], writes=[mvB2])
                P.op("act", lambda e: e.activation(out=mv[:, 2:3], in_=mv[:, 2:3], func=AF.Sqrt),
                     reads=[mvB2], writes=[mvB2])
                P.op("dve", lambda e: e.reciprocal(out=mv[:, 2:3], in_=mv[:, 2:3]), reads=[mvB2], writes=[mvB2])
                P.op("dve", lambda e: e.scalar_tensor_tensor(out=mv[:, 3:4], in0=mv[:, 0:1], scalar=-1.0, in1=mv[:, 2:3],
                                                             op0=ALU.mult, op1=ALU.mult), reads=[mvB2], writes=[mvB2])
                P.op("act", lambda e: e.activation(out=rb[:, :], in_=rb[:, :], func=AF.Identity,
                                                   bias=mv[:, 3:4], scale=mv[:, 2:3]),
                     reads=[rbB, mvB2], writes=[rbB])

            def fb():
                P.op("dve", lambda e: e.tensor_tensor(out=rb[:, :], in0=rb[:, :], in1=gout[:, :], op=ALU.mult),
                     reads=[rbB, constB], writes=[rbB])
                P.op("pool", lambda e: e.tensor_tensor(out=rb[:, :], in0=rb[:, :], in1=bout[:, :], op=ALU.add),
                     reads=[rbB, constB], writes=[rbB])
                P.dma("pool", lambda e: e.dma_start(
                    out=out_own[j * CH + tt * 128:j * CH + (tt + 1) * 128, :], in_=rb[:, :]), reads=[rbB])
            return fa, fb
        skew([final_tile(tt) for tt in range(4)])

    Epool = engs["pool"]
    for q in dsems:
        for s in dsems[q]:
            if s["total"] > 0 and Epool.waited.get(s["key"], 0) < s["total"]:
                Epool.ops.append(("wait", s["sem"], s["total"]))

    with nc.Block() as block:
        @block.tensor
        def _(e):
            _run(e, engs["pe"])

        @block.scalar
        def _(e):
            _run(e, engs["act"])

        @block.vector
        def _(e):
            _run(e, engs["dve"])

        @block.gpsimd
        def _(e):
            _run(e, engs["pool"])

        @block.sync
        def _(e):
            _run(e, engs["sp"])

    es.close()
    return nc


def _host_tables():
    f = np.float32
    ident = np.eye(128, dtype=f)
    blk = np.zeros((128, 128), f)
    blk[:64, :64] = 1.0
    blk[64:, 64:] = 1.0
    kk = np.arange(128)[:, None]
    qq = np.arange(128)[None, :]
    mask = (kk <= qq).astype(f)
    slopes = np.array([2.0 ** (-(h + 1)) for h in range(8)], dtype=np.float64)
    tk = np.zeros((128, 4, 128), f)
    tq = np.zeros((128, 4, 512), f)
    qi = np.arange(512)
    for hp in range(4):
        for g in range(4):
            h = 2 * hp + g // 2
            tk[32 * g + 0, hp, :] = slopes[h] * np.arange(128)
            tk[32 * g + 1, hp, :] = 1.0
            tk[32 * g + 2, hp, :] = 1.0
            tq[32 * g + 0, hp, :] = 1.0
            tq[32 * g + 1, hp, :] = -slopes[h] * 128.0 * (qi // 128)
            tq[32 * g + 2, hp, :] = -slopes[h] * (qi % 128)
    return ident, blk, mask, tk.reshape(128, 512), tq.reshape(128, 2048), slopes


def _core_tables(r, slopes):
    f = np.float32
    own = OWN[r]
    oth = OWN[1 - r]
    bm = np.zeros((NSLOT, 128, 16, 4, 8), f)
    hv = np.zeros((128, 8), f)
    invc = np.zeros((128, 8, 4, 16), f)
    for j in range(NSLOT):
        c = own[j]
        entries = [own[i] for i in range(j)] + [oth[i] for i in range(j + 1)] + [c]
        for ei, kc in enumerate(entries):
            for kt in range(4):
                if ei == len(entries) - 1:
                    m = kt
                    valid = True
                else:
                    m = (512 * kc + 128 * kt - 512 * c) // 128
                    valid = kc < c
                for h in range(8):
                    if not valid:
                        bm[j, :, ei, kt, h] = NEG_BIG
                    elif h < 2:
                        bm[j, :, ei, kt, h] = slopes[h] * 128.0 * m
                    else:
                        bm[j, :, ei, kt, h] = slopes[h] * (np.arange(128) + 128.0 * m - 256.0)
        hv[:, j] = 0.0 if c == 0 else 1.0
        for g, w in enumerate((2, 4, 8, 16)):
            pos = 512 * c + np.arange(16)
            invc[:, j, g, :] = (1.0 / np.minimum(pos + 1, w)).astype(f)[None, :]
    return bm.reshape(NSLOT, 128, 512), hv, invc.reshape(128, 512)


_NC_CACHE = {}


def kernel(x, mem, ln_in_g, ln_in_b, w_in, b_gate, pool_w, pool_scale, lambda_q1, lambda_k1,
           lambda_q2, lambda_k2, diff_norm_g, w_mem_kv, w_branch, w_out, ln_out_g, ln_out_b):
    f = np.float32
    x = np.asarray(x, f)
    mem = np.asarray(mem, f)
    ident, blk, mask, tk, tq, slopes = _host_tables()
    rep = lambda v: np.ascontiguousarray(np.broadcast_to(np.asarray(v, f).reshape(1, -1), (128, np.asarray(v).size)))
    lam4 = np.concatenate([rep(lambda_q1), rep(lambda_k1), rep(lambda_q2), rep(lambda_k2)], axis=1)
    common = {
        "w_in": np.ascontiguousarray(np.asarray(w_in, f)[0]),
        "w_kv": np.ascontiguousarray(np.asarray(w_mem_kv, f)[0]),
        "w_br": np.ascontiguousarray(np.asarray(w_branch, f)[0].reshape(1536, D)),
        "w_out": np.ascontiguousarray(np.asarray(w_out, f)[0]),
        "poolw": np.ascontiguousarray(np.asarray(pool_w, f)[0].transpose(1, 0, 2).reshape(128, 512)),
        "g_in": rep(ln_in_g), "b_in": rep(ln_in_b),
        "g_fm": np.ascontiguousarray(np.asarray(ln_in_g, f).reshape(8, 128).T),
        "b_fm": np.ascontiguousarray(np.asarray(ln_in_b, f).reshape(8, 128).T),
        "g_out": rep(np.asarray(ln_out_g)[0]), "b_out": rep(np.asarray(ln_out_b)[0]),
        "bgate": np.ascontiguousarray(np.asarray(b_gate, f)[0].reshape(24, 128).T),
        "pscale": np.ascontiguousarray(np.asarray(pool_scale, f)[0].reshape(4, 128).T),
        "dng": np.ascontiguousarray(np.tile(np.asarray(diff_norm_g, f)[0], 2).reshape(128, 1)),
        "lam4": np.ascontiguousarray(lam4),
        "ident": ident, "blk": blk, "mask": mask, "tk": tk, "tq": tq,
    }
    in_maps = []
    for c in range(NCORES):
        b, r = c // 2, c % 2
        own, oth = OWN[r], OWN[1 - r]
        xb = x[b].reshape(16, CH, D)
        halo = np.zeros((NSLOT, HALO, D), f)
        for j, cj in enumerate(own):
            if cj > 0:
                halo[j] = x[b, cj * CH - HALO:cj * CH, :]
        bm, hv, invc = _core_tables(r, slopes)
        m = dict(common)
        m["x_own"] = np.ascontiguousarray(xb[own].reshape(NSLOT * CH, D))
        m["x_oth"] = np.ascontiguousarray(xb[oth].reshape(NSLOT * CH, D))
        m["x_halo"] = halo.reshape(NSLOT * HALO, D)
        m["memT"] = np.ascontiguousarray(mem[b].T)
        m["bm"] = bm
        m["hv"] = hv
        m["invc"] = invc
        in_maps.append(m)
    if "nc" not in _NC_CACHE:
        _NC_CACHE["nc"] = build()
    nc = _NC_CACHE["nc"]
    resu = run_bass_kernel_spmd(nc, in_maps, core_ids=list(range(NCORES)))
    out = np.zeros((BATCH, 16, CH, D), f)
    for c in range(NCORES):
        b, r = c // 2, c % 2
        o = np.asarray(resu.results[c]["out_own"], f).reshape(NSLOT, CH, D)
        for j, cj in enumerate(OWN[r]):
            out[b, cj] = o[j]
    return out.reshape(BATCH, SEQ, D)
```

```python
import contextlib
import numpy as np
import concourse.bass as bass
import concourse.mybir as mybir
from concourse.bass_utils import run_bass_kernel_spmd

F32 = mybir.dt.float32
BF16 = mybir.dt.bfloat16
AF = mybir.ActivationFunctionType
ALU = mybir.AluOpType
AXX = mybir.AxisListType.X

D = 1024
SEQ = 8192
BATCH = 4
NCORES = 8
CH = 512
NSLOT = 8
HALO = 128
OWN = ([0, 3, 4, 7, 8, 11, 12, 15], [1, 2, 5, 6, 9, 10, 13, 14])
ALPHA = 2.0 ** 0.25
LAM_INIT = 0.8 - 0.6 * 1.0
LN_EPS = 1e-5
RMS_EPS = 1e-5
NEG_BIG = -30000.0
ND = 24
SLOPES = [2.0 ** (-(h + 1)) for h in range(8)]
SKIP_THRESH = 144.0

RUN_SLOTS = NSLOT


class Buf:
    __slots__ = ("w", "r")

    def __init__(self):
        self.w = None
        self.r = {}


class MultiBuf:
    def __init__(self):
        self.bufs = []

    def new(self):
        b = Buf()
        self.bufs.append(b)
        return b


def _flat(bufs):
    out = []
    for b in bufs:
        if isinstance(b, MultiBuf):
            out.extend(b.bufs)
        else:
            out.append(b)
    return out


class Eng:
    def __init__(self, name, sem):
        self.name = name
        self.sem = sem
        self.cnt = 0
        self.ops = []
        self.waited = {}


class Prog:
    def __init__(self, engs, dsems):
        self.eng = engs
        self.dsems = dsems
        self.di = {q: 0 for q in dsems}

    def _deps(self, reads, writes):
        deps = []
        for b in reads:
            if b.w is not None:
                deps.append((b.w, True))
        for b in writes:
            if b.w is not None:
                deps.append((b.w, False))
            for ev in b.r.values():
                deps.append((ev, False))
        return deps

    def _emit_waits(self, E, deps, is_dma):
        for (ev, raw) in deps:
            key, sem, val = ev
            if (not is_dma) and key == E.name:
                if E.name == "pe":
                    continue
            if E.waited.get(key, 0) >= val:
                continue
            E.waited[key] = val
            E.ops.append(("wait", sem, val))

    def op(self, en, fn, reads=(), writes=()):
        reads, writes = _flat(reads), _flat(writes)
        E = self.eng[en]
        self._emit_waits(E, self._deps(reads, writes), False)
        E.cnt += 1
        ev = (en, E.sem, E.cnt)
        E.ops.append(("op", fn))
        for b in reads:
            b.r[en] = ev
        for b in writes:
            b.w = ev
            b.r = {}

    def dma(self, qn, fn, reads=(), writes=()):
        reads, writes = _flat(reads), _flat(writes)
        Q = self.eng[qn]
        pool = self.dsems[qn]
        s = pool[self.di[qn] % len(pool)]
        self.di[qn] += 1
        deps = self._deps(reads, writes)
        if s["total"] > 0:
            deps.append(((s["key"], s["sem"], s["total"]), True))
        self._emit_waits(Q, deps, True)
        s["total"] += 16
        ev = (s["key"], s["sem"], s["total"])
        Q.ops.append(("dma", fn, s["sem"]))
        for b in reads:
            b.r[s["key"]] = ev
        for b in writes:
            b.w = ev
            b.r = {}


def _run(e, E):
    for o in E.ops:
        if o[0] == "wait":
            e.wait_ge(o[1], o[2])
        elif o[0] == "op":
            o[1](e).then_inc(E.sem, 1)
        else:
            o[1](e).then_inc(o[2], 16)


class Ring:
    def __init__(self, tiles):
        self.tiles = tiles
        self.bufs = [Buf() for _ in tiles]
        self.i = 0

    def next(self):
        k = self.i % len(self.tiles)
        self.i += 1
        return self.tiles[k], self.bufs[k]


def build():
    nc = bass.Bass("TRN2", target_bir_lowering=False)
    es = contextlib.ExitStack()

    def din(name, shape, dt=F32):
        return nc.dram_tensor(name, list(shape), dt, kind="ExternalInput").ap()

    x_own = din("x_own", [NSLOT * CH, D])
    x_oth = din("x_oth", [NSLOT * CH, D])
    x_halo = din("x_halo", [NSLOT * HALO, D])
    memT = din("memT", [D, 256])
    w_in = din("w_in", [D, 7168])
    w_kv = din("w_kv", [D, 1024])
    w_br = din("w_br", [1536, D])
    w_out = din("w_out", [D, D])
    poolw = din("poolw", [128, 512])
    g_in = din("g_in", [128, D])
    b_in = din("b_in", [128, D])
    g_out = din("g_out", [128, D])
    b_out = din("b_out", [128, D])
    g_fm = din("g_fm", [128, 8])
    b_fm = din("b_fm", [128, 8])
    bgate = din("bgate", [128, 24])
    pscale = din("pscale", [128, 4])
    dng = din("dng", [128, 1])
    lam4 = din("lam4", [128, 128])
    ident_d = din("ident", [128, 128])
    blk_d = din("blk", [128, 128])
    mask_d = din("mask", [128, 128])
    tk_d = din("tk", [128, 512])
    tq_d = din("tq", [128, 2048])
    bm_d = din("bm", [NSLOT, 128, 512])
    hv_d = din("hv", [128, 8])
    invc_d = din("invc", [128, 512])
    out_own = nc.dram_tensor("out_own", [NSLOT * CH, D], F32, kind="ExternalOutput").ap()

    wbf_in = nc.dram_tensor("wbf_in", [D, 7168], BF16).ap()
    wbf_kv = nc.dram_tensor("wbf_kv", [D, 1024], BF16).ap()
    wbf_br = nc.dram_tensor("wbf_br", [1536, D], BF16).ap()
    wbf_out = nc.dram_tensor("wbf_out", [D, D], BF16).ap()
    KTd = {s: nc.dram_tensor("ktd_" + s, [4, 128, NSLOT * CH], BF16).ap() for s in ("own", "oth")}
    Vd = {s: nc.dram_tensor("vd_" + s, [NSLOT * 4, 128, 512], BF16).ap() for s in ("own", "oth")}
    KTd_B = {s: [Buf() for _ in range(NSLOT)] for s in ("own", "oth")}
    Vd_B = {s: [Buf() for _ in range(NSLOT)] for s in ("own", "oth")}
    WB = {k: Buf() for k in ("in", "kv", "br", "out")}

    def sb(name, shape, dt):
        return es.enter_context(nc.sbuf_tensor("sb_" + name, list(shape), dt))

    def sem(name):
        return es.enter_context(nc.semaphore(name))

    engs = {n: Eng(n, sem("s_" + n)) for n in ("pe", "act", "dve", "pool", "sp")}
    dsems = {q: [dict(key=("d" + q, i), sem=sem("d%s%d" % (q, i)), total=0) for i in range(n)]
             for q, n in (("sp", 16), ("pool", 12))}
    P = Prog(engs, dsems)

    ps = es.enter_context(nc.psum_tensor("ps", [128, 4096], F32))
    bankB = [Buf() for _ in range(8)]
    bank_ctr = [0]

    def nb():
        k = bank_ctr[0] % 8
        bank_ctr[0] += 1
        return k

    def bank(k, n=512, p0=0, p1=128):
        return ps[p0:p1, k * 512:k * 512 + n]

    def bank_bf(k):
        return ps[:, k * 512:(k + 1) * 512].bitcast(BF16)

    ident = sb("ident", [128, 128], BF16)
    blk = sb("blk", [128, 128], F32)
    maskb = sb("maskb", [128, 128], BF16)
    ones = sb("ones", [128, 128], BF16)
    TK = sb("TK", [128, 128], BF16)
    TQ = sb("TQ", [128, 512], BF16)
    gin = sb("gin", [128, D], F32)
    bin_ = sb("bin", [128, D], F32)
    gout = sb("gout", [128, D], F32)
    bout = sb("bout", [128, D], F32)
    bg = sb("bg", [128, 24], F32)
    gfm = sb("gfm", [128, 8], F32)
    bfm = sb("bfm", [128, 8], F32)
    psc = sb("psc", [128, 4], F32)
    gsc = sb("gsc", [128, 1], F32)
    lamt = sb("lamt", [128, 128], F32)
    lamw = sb("lamw", [128, 8], F32)
    hv = sb("hv", [128, 8], F32)
    invc = sb("invc", [128, 512], F32)
    pwb = sb("pwb", [128, 512], BF16)
    mkT = sb("mkT", [128, 4, 256], BF16)
    mvb = sb("mvb", [128, 2, 512], BF16)
    constB = MultiBuf()
    constB.new()
    mkB, mvB_ = Buf(), Buf()

    NW = 3
    wring = Ring([sb("wr%d" % i, [128, 8, 512], BF16) for i in range(NW)])
    wbring = Ring([sb("wbr%d" % i, [128, 4, 512], BF16) for i in range(3)])
    xin = Ring([sb("xin%d" % i, [128, D], F32) for i in range(2)])
    xhb = Ring([sb("xhb%d" % i, [128, D], BF16) for i in range(2)])
    stt = Ring([sb("stt%d" % i, [128, 12], F32) for i in range(2)])
    mvr = Ring([sb("mvr%d" % i, [128, 4], F32) for i in range(2)])

    xT = sb("xT", [128, 8, CH], BF16)
    xTB = Buf()
    xTh = sb("xTh", [128, 8, HALO], BF16)
    xThB = Buf()
    res = sb("res", [128, 4, D], F32)
    stage = res[:, 0:2, :].rearrange("p a b -> p (a b)")
    resB = [Buf() for _ in range(4)]
    QT = sb("QT", [128, 4, CH], BF16)
    QTB = Buf()
    mqT = sb("mqT", [128, 4, CH], BF16)
    mqB = Buf()
    sz = {k: sb("sz_" + k, [128, 4, CH], BF16) for k in ("pz", "dz", "mz")}
    szB = {k: Buf() for k in ("pz", "dz", "mz")}
    uT = sb("uT", [128, 4, HALO + CH], F32)
    uTB = [Buf() for _ in range(4)]
    pa = sb("pa", [128, HALO + CH], F32)
    pb = sb("pb", [128, HALO + CH], F32)
    paB, pbB = Buf(), Buf()
    pooled = Ring([sb("pooled%d" % i, [128, CH], BF16) for i in range(2)])
    obr = {k: sb("o_" + k, [128, 4, CH], BF16) for k in ("pool", "diff", "mem")}
    obrB = {k: Buf() for k in ("pool", "diff", "mem")}
    kst = Ring([sb("kst%d" % i, [128, CH], BF16) for i in range(2)])
    pm = Ring([sb("pm%d" % i, [128, 2, CH], BF16) for i in range(2)])
    ftmp = Ring([sb("ftmp%d" % i, [128, CH], F32) for i in range(4)])
    gsig = Ring([sb("gsig%d" % i, [128, CH], BF16) for i in range(3)])
    merged = sb("merged", [128, 8, CH], BF16)
    mergedB = Buf()
    rbuf = Ring([sb("rbuf%d" % i, [128, D], F32) for i in range(3)])
    kbufs = Ring([sb("kbuf%d" % i, [128, CH], BF16) for i in range(2)])
    vbufs = Ring([sb("vbuf%d" % i, [128, 4, 128], BF16) for i in range(2)])
    PTall = sb("ptall", [128, 3, 2048], BF16)
    PTs = [PTall[:, i, :].rearrange("p (a b) -> p a b", a=2) for i in range(3)]
    PTB = [[Buf(), Buf()] for _ in range(3)]
    xTo = PTall[:, 0:2, :].rearrange("p a b -> p (a b)").rearrange("p (kc t) -> p kc t", kc=8)
    xToB = MultiBuf()
    xToB.new()
    xToB.bufs.extend([PTB[0][0], PTB[0][1], PTB[1][0], PTB[1][1]])
    bmr = Ring([sb("bmr%d" % i, [128, 512], F32) for i in range(2)])

    def load_const(dst, src, cast_via=None):
        cb = constB.new()
        if cast_via is None:
            P.dma("sp", lambda e, d=dst, s=src: e.dma_start(out=d, in_=s), writes=[cb])
        else:
            sb_ = Buf()
            P.dma("sp", lambda e, d=cast_via, s=src: e.dma_start(out=d, in_=s), writes=[sb_])
            P.op("dve", lambda e, d=dst, s=cast_via: e.tensor_copy(out=d, in_=s),
                 reads=[sb_], writes=[cb, resB[0], resB[1]])

    WBg = {}

    def cast_w(dst, src, key, r0, nrows, c0, ncols):
        b = Buf()
        for c in range(c0, c0 + ncols, 512):
            WBg[(key, r0, c)] = b
        P.dma("pool", lambda e, d=dst[r0:r0 + nrows, c0:c0 + ncols], s=src[r0:r0 + nrows, c0:c0 + ncols]:
              e.dma_start(out=d, in_=s), writes=[b])

    cast_w(wbf_in, w_in, "in", 0, D, 1536, 1024)
    cast_w(wbf_in, w_in, "in", 0, D, 0, 1536)
    cast_w(wbf_in, w_in, "in", 0, D, 2560, 1536)
    cast_w(wbf_kv, w_kv, "kv", 0, D, 0, 1024)
    cast_w(wbf_in, w_in, "in", 0, D, 4096, 3072)
    for n in range(3):
        cast_w(wbf_br, w_br, "br", n * 512, 512, 0, 1024)
    cast_w(wbf_out, w_out, "out", 0, D, 0, 1024)

    def cast_batch1():
        pass

    def cast_batch2():
        pass

    load_const(ident[:, :], ident_d, stage[:, 0:128])
    load_const(maskb[:, :], mask_d, stage[:, 128:256])
    load_const(TK[:, :], tk_d[:, 0:128], stage[:, 256:384])
    load_const(pwb[:, :], poolw, stage[:, 768:1280])
    load_const(blk[:, :], blk_d)
    load_const(gin[:, :], g_in)
    load_const(bin_[:, :], b_in)
    load_const(gout[:, :], g_out)
    load_const(bout[:, :], b_out)
    load_const(bg[:, :], bgate)
    load_const(gfm[:, :], g_fm)
    load_const(bfm[:, :], b_fm)
    load_const(psc[:, :], pscale)
    load_const(gsc[:, :], dng)
    load_const(lamt[:, :], lam4)
    load_const(hv[:, :], hv_d)
    load_const(invc[:, :], invc_d)
    load_const(TQ[:, :], tq_d[:, 0:512], stage[:, 1280:1792])
    P.op("dve", lambda e: e.memset(ones[:, :], 1.0), writes=[constB])
    P.op("dve", lambda e: e.tensor_scalar(out=gin[:, :], in0=gin[:, :], scalar1=ALPHA, scalar2=None, op0=ALU.mult),
         reads=[constB], writes=[constB])
    P.op("dve", lambda e: e.tensor_scalar(out=bin_[:, :], in0=bin_[:, :], scalar1=ALPHA, scalar2=None, op0=ALU.mult),
         reads=[constB], writes=[constB])
    P.op("dve", lambda e: e.tensor_scalar(out=gsc[:, :], in0=gsc[:, :], scalar1=(1.0 - LAM_INIT) * 8.0, scalar2=None,
                                          op0=ALU.mult), reads=[constB], writes=[constB])
    P.op("dve", lambda e: e.tensor_tensor(out=lamt[:, 0:32], in0=lamt[:, 0:32], in1=lamt[:, 32:64], op=ALU.mult),
         reads=[constB], writes=[constB])
    P.op("dve", lambda e: e.tensor_tensor(out=lamt[:, 64:96], in0=lamt[:, 64:96], in1=lamt[:, 96:128], op=ALU.mult),
         reads=[constB], writes=[constB])
    P.op("dve", lambda e: e.reduce_sum(out=lamw[:, 0:1], in_=lamt[:, 0:32], axis=AXX), reads=[constB], writes=[constB])
    P.op("dve", lambda e: e.reduce_sum(out=lamw[:, 1:2], in_=lamt[:, 64:96], axis=AXX), reads=[constB], writes=[constB])
    P.op("act", lambda e: e.activation(out=lamw[:, 2:4], in_=lamw[:, 0:2], func=AF.Exp), reads=[constB], writes=[constB])
    P.op("dve", lambda e: e.tensor_tensor(out=lamw[:, 4:5], in0=lamw[:, 3:4], in1=lamw[:, 2:3], op=ALU.subtract),
         reads=[constB], writes=[constB])
    P.op("dve", lambda e: e.tensor_scalar(out=lamw[:, 5:6], in0=lamw[:, 4:5], scalar1=-LAM_INIT, scalar2=None,
                                          op0=ALU.add), reads=[constB], writes=[constB])
    neglam = lamw[:, 5:6]

    def wsrc(w, r0, nkc, c0):
        return w[r0:r0 + nkc * 128, c0:c0 + 512].rearrange("(kc p) c -> p kc c", p=128)

    def load_w(w, key, r0, nkc, c0, dst=None, dstB=None):
        if dst is None:
            dst, dstB = wring.next()
        P.dma("sp", lambda e, d=dst[:, 0:nkc, :], s=wsrc(w, r0, nkc, c0): e.dma_start(out=d, in_=s),
              reads=[WBg[(key, r0, c0)]], writes=[dstB])
        return dst, dstB


    def mm_group(out_ap, pairs, bufs_r, bufB, tp=None):
        def f(e, out_ap=out_ap, pairs=pairs):
            n = len(pairs)
            ins = None
            for i, (l, r) in enumerate(pairs):
                ins = e.matmul(out_ap, lhsT=l, rhs=r, start=(i == 0), stop=(i == n - 1))
            return ins
        P.op("pe", f, reads=bufs_r, writes=[bufB])

    def mem_kv():
        halves = []
        for hh in range(2):
            stg, stgB = rbuf.next()
            P.dma("sp", lambda e, stg=stg, hh=hh: e.dma_start(
                out=stg[:, :].rearrange("p (kc m) -> p kc m", kc=4),
                in_=memT[hh * 512:(hh + 1) * 512, :].rearrange("(kc p) m -> p kc m", p=128)), writes=[stgB])
            halves.append((stg, stgB))
        for hh, (stg, stgB) in enumerate(halves):
            P.op("dve", lambda e, stg=stg, hh=hh: e.tensor_copy(
                out=xTo[:, 4 * hh:4 * hh + 4, 0:256], in_=stg[:, :].rearrange("p (kc m) -> p kc m", kc=4)),
                reads=[stgB], writes=[xToB])
        wmk, wmkB = load_w(wbf_kv, "kv", 0, 8, 0)
        wmv, wmvB = load_w(wbf_kv, "kv", 0, 8, 512)
        for hd in range(4):
            k = nb()
            mm_group(bank(k, 256), [(wmk[:, kc, hd * 128:(hd + 1) * 128], xTo[:, kc, 0:256]) for kc in range(8)],
                     [wmkB, xToB], bankB[k])
            P.op("dve", lambda e, k=k, hd=hd: e.tensor_copy(out=mkT[:, hd, :], in_=bank(k, 256)),
                 reads=[bankB[k]], writes=[mkB])
        for mt in range(2):
            k = nb()
            mm_group(bank(k), [(xTo[:, kc, mt * 128:(mt + 1) * 128], wmv[:, kc, :]) for kc in range(8)],
                     [wmvB, xToB], bankB[k])
            P.op("dve", lambda e, k=k, mt=mt: e.tensor_copy(out=mvb[:, mt, :], in_=bank(k)),
                 reads=[bankB[k]], writes=[mvB_])

    def skew(stages):
        prev = None
        for pa_, pb_ in stages:
            pa_()
            if prev is not None:
                prev()
            prev = pb_
        if prev is not None:
            prev()

    def ln_tile(src_rows, nrows, dstT, dstTB, col0, keep=None):
        xt, xB = xin.next()
        st, stB = stt.next()
        mv, mvB2 = mvr.next()
        k = nb()
        bt = bank_bf(k)
        hb, hbB = xhb.next()
        btv = bt.rearrange("p (kc t) -> p kc t", kc=8)[:, :, 0:nrows]

        def tr(e):
            ins = None
            for kc in range(8):
                ins = e.transpose(out=bt[:, kc * 128:kc * 128 + nrows], in_=hb[0:nrows, kc * 128:(kc + 1) * 128],
                                  identity=ident[0:nrows, 0:nrows])
            return ins

        def stage_a():
            P.dma("sp", lambda e: e.dma_start(out=xt[0:nrows, :], in_=src_rows), writes=[xB])
            P.op("dve", lambda e: e.bn_stats(out=st[0:nrows, 0:6], in_=xt[0:nrows, 0:512]), reads=[xB], writes=[stB])
            P.op("dve", lambda e: e.bn_stats(out=st[0:nrows, 6:12], in_=xt[0:nrows, 512:1024]), reads=[xB], writes=[stB])
            P.op("dve", lambda e: e.bn_aggr(out=mv[0:nrows, 0:2], in_=st[0:nrows, 0:12]), reads=[stB], writes=[mvB2])
            P.op("dve", lambda e: e.tensor_scalar(out=mv[0:nrows, 2:3], in0=mv[0:nrows, 1:2], scalar1=LN_EPS,
                                                  scalar2=None, op0=ALU.add), reads=[mvB2], writes=[mvB2])
            P.op("act", lambda e: e.activation(out=mv[0:nrows, 2:3], in_=mv[0:nrows, 2:3], func=AF.Sqrt),
                 reads=[mvB2], writes=[mvB2])
            P.op("dve", lambda e: e.reciprocal(out=mv[0:nrows, 2:3], in_=mv[0:nrows, 2:3]), reads=[mvB2], writes=[mvB2])
            P.op("dve", lambda e: e.scalar_tensor_tensor(out=mv[0:nrows, 3:4], in0=mv[0:nrows, 0:1], scalar=-1.0,
                                                         in1=mv[0:nrows, 2:3], op0=ALU.mult, op1=ALU.mult),
                 reads=[mvB2], writes=[mvB2])
            if keep is None:
                P.op("act", lambda e: e.activation(out=hb[0:nrows, :], in_=xt[0:nrows, :], func=AF.Identity,
                                                   bias=mv[0:nrows, 3:4], scale=mv[0:nrows, 2:3]),
                     reads=[xB, mvB2], writes=[hbB])
            else:
                rt, rB = keep
                P.op("act", lambda e: e.activation(out=rt, in_=xt[0:nrows, :], func=AF.Identity, bias=mv[0:nrows, 3:4],
                                                   scale=mv[0:nrows, 2:3]), reads=[xB, mvB2], writes=[rB])

        def stage_b():
            if keep is None:
                P.op("pe", tr, reads=[hbB, constB], writes=[bankB[k]])
                tmp, tmpB = rbuf.next()
                tv = tmp[:, :].rearrange("p (kc t) -> p kc t", kc=8)[:, :, 0:nrows]
                P.op("dve", lambda e: e.tensor_tensor(out=tv, in0=btv,
                                                      in1=gfm[:, :].unsqueeze(2).to_broadcast([128, 8, nrows]),
                                                      op=ALU.mult), reads=[bankB[k], constB], writes=[tmpB])
                P.op("dve", lambda e: e.tensor_tensor(out=dstT[:, :, col0:col0 + nrows], in0=tv,
                                                      in1=bfm[:, :].unsqueeze(2).to_broadcast([128, 8, nrows]),
                                                      op=ALU.add), reads=[tmpB, constB], writes=[dstTB])
            else:
                rt, rB = keep
                P.op("dve", lambda e: e.tensor_tensor(out=rt, in0=rt, in1=gin[0:nrows, :], op=ALU.mult),
                     reads=[rB, constB], writes=[rB])
                P.op("pool", lambda e: e.tensor_tensor(out=rt, in0=rt, in1=bin_[0:nrows, :], op=ALU.add),
                     reads=[rB, constB], writes=[rB])
                P.op("dve", lambda e: e.tensor_scalar(out=hb[0:nrows, :], in0=rt, scalar1=1.0 / ALPHA, scalar2=None,
                                                      op0=ALU.mult), reads=[rB], writes=[hbB])
                P.op("pe", tr, reads=[hbB, constB], writes=[bankB[k]])
                P.op("act", lambda e: e.activation(out=dstT[:, :, col0:col0 + nrows], in_=btv, func=AF.Copy),
                     reads=[bankB[k]], writes=[dstTB])
        return stage_a, stage_b

    def kv_project(src_xT, src_B, store, j, part=None):
        if part in (None, "k"):
            wk, wkB = load_w(wbf_in, "in", 0, 8, 3 * 512)
        if part in (None, "v"):
            wv, wvB = load_w(wbf_in, "in", 0, 8, 4 * 512)
        for hp in (range(4) if part in (None, "k") else ()):
            k = nb()
            mm_group(bank(k), [(wk[:, kc, hp * 128:(hp + 1) * 128], src_xT[:, kc, :]) for kc in range(8)],
                     [wkB, src_B], bankB[k])
            ks, ksB = kst.next()
            P.op("act", lambda e, k=k, ks=ks: e.activation(out=ks[:, :], in_=bank(k), func=AF.Copy),
                 reads=[bankB[k]], writes=[ksB])
            P.dma("pool", lambda e, ks=ks, hp=hp: e.dma_start(out=KTd[store][hp, :, j * CH:(j + 1) * CH], in_=ks[:, :]),
                  reads=[ksB], writes=[KTd_B[store][j]])
        for tt in (range(4) if part in (None, "v") else ()):
            k = nb()
            mm_group(bank(k), [(src_xT[:, kc, tt * 128:(tt + 1) * 128], wv[:, kc, :]) for kc in range(8)],
                     [wvB, src_B], bankB[k])
            ks, ksB = kst.next()
            P.op("dve", lambda e, k=k, ks=ks: e.tensor_copy(out=ks[:, :], in_=bank(k)),
                 reads=[bankB[k]], writes=[ksB])
            P.dma("pool", lambda e, ks=ks, tt=tt: e.dma_start(out=Vd[store][j * 4 + tt, :, :], in_=ks[:, :]),
                  reads=[ksB], writes=[Vd_B[store][j]])

    def fm_group(wt, wtB, ct, src_xT, src_B, n=CH):
        k = nb()
        mm_group(bank(k, n), [(wt[:, kc, ct * 128:(ct + 1) * 128], src_xT[:, kc, 0:n]) for kc in range(8)],
                 [wtB, src_B], bankB[k])
        return k

    def stage_A(jj):
        skew([ln_tile(x_oth[jj * CH + tt * 128:jj * CH + (tt + 1) * 128, :], 128, xTo, xToB, tt * 128)
              for tt in range(4)])
        kv_project(xTo, xToB, "oth", jj)

    for j in range(RUN_SLOTS):
        bmt, bmB = bmr.next()
        P.dma("sp", lambda e, bmt=bmt, j=j: e.dma_start(out=bmt[:, :], in_=bm_d[j, :, :]), writes=[bmB])

        if j == 0:
            stage_A(0)
            cast_batch1()

        skew([ln_tile(x_own[j * CH + tt * 128:j * CH + (tt + 1) * 128, :], 128, xT, xTB, tt * 128,
                      keep=(res[:, tt, :], resB[tt])) for tt in range(4)]
             + [ln_tile(x_halo[j * HALO:(j + 1) * HALO, :], HALO, xTh, xThB, 0)])
        if j == 0:
            cast_batch2()
        kv_project(xT, xTB, "own", j)

        wt, wtB = load_w(wbf_in, "in", 0, 8, 0)
        for g in range(4):
            k = fm_group(wt, wtB, g, xT, xTB)
            P.op("dve", lambda e, k=k, g=g: e.tensor_copy(out=uT[:, g, HALO:HALO + CH], in_=bank(k)),
                 reads=[bankB[k]], writes=[uTB[g]])
            k = fm_group(wt, wtB, g, xTh, xThB, n=HALO)
            P.op("dve", lambda e, k=k, g=g, j=j: e.tensor_scalar(out=uT[:, g, 0:HALO], in0=bank(k, HALO),
                                                                 scalar1=hv[:, j:j + 1], scalar2=None, op0=ALU.mult),
                 reads=[bankB[k], constB], writes=[uTB[g]])
        def proj_silu(grp, kind):
            wt, wtB = load_w(wbf_in, "in", 0, 8, grp * 512)
            for ct in range(4):
                k = fm_group(wt, wtB, ct, xT, xTB)
                P.op("act", lambda e, k=k, ct=ct, kind=kind: e.activation(out=sz[kind][:, ct, :], in_=bank(k),
                                                                          func=AF.Silu),
                     reads=[bankB[k]], writes=[szB[kind]])
        proj_silu(1, "pz")
        W = HALO + CH
        for g, w in enumerate((2, 4, 8, 16)):
            U = uT[:, g, :]
            P.op("pool", lambda e, U=U: e.tensor_tensor(out=pa[:, 1:W], in0=U[:, 1:W], in1=U[:, 0:W - 1], op=ALU.add),
                 reads=[uTB[g]], writes=[paB])
            cur, curB = pa, paB
            if w >= 4:
                P.op("pool", lambda e: e.tensor_tensor(out=pb[:, 3:W], in0=pa[:, 3:W], in1=pa[:, 1:W - 2], op=ALU.add),
                     reads=[paB], writes=[pbB])
                cur, curB = pb, pbB
            if w >= 8:
                P.op("pool", lambda e: e.tensor_tensor(out=pa[:, 7:W], in0=pb[:, 7:W], in1=pb[:, 3:W - 4], op=ALU.add),
                     reads=[pbB], writes=[paB])
                cur, curB = pa, paB
            if w >= 16:
                P.op("pool", lambda e: e.tensor_tensor(out=pb[:, 15:W], in0=pa[:, 15:W], in1=pa[:, 7:W - 8], op=ALU.add),
                     reads=[paB], writes=[pbB])
                cur, curB = pb, pbB
            pl, plB = pooled.next()
            P.op("dve", lambda e, cur=cur, U=U, w=w, pl=pl: e.scalar_tensor_tensor(
                out=pl[:, :], in0=cur[:, HALO:W], scalar=1.0 / w, in1=U[:, HALO:W], op0=ALU.mult, op1=ALU.subtract),
                reads=[curB, uTB[g]], writes=[plB])
            ic = invc[:, (j * 4 + g) * 16:(j * 4 + g + 1) * 16]
            P.op("pool", lambda e, cur=cur, ic=ic: e.tensor_tensor(out=cur[:, 0:16], in0=cur[:, HALO:HALO + 16], in1=ic, op=ALU.mult),
                 reads=[curB, constB, plB], writes=[curB])
            P.op("pool", lambda e, cur=cur, U=U, pl=pl: e.tensor_tensor(out=pl[:, 0:16], in0=cur[:, 0:16], in1=U[:, HALO:HALO + 16],
                                                                         op=ALU.subtract),
                 reads=[curB, uTB[g]], writes=[plB])
            k = nb()
            mm_group(bank(k), [(pwb[:, g * 128:(g + 1) * 128], pl[:, :])], [plB, constB], bankB[k])
            P.op("dve", lambda e, k=k, g=g: e.scalar_tensor_tensor(
                out=obr["pool"][:, g, :], in0=bank(k), scalar=psc[:, g:g + 1], in1=sz["pz"][:, g, :],
                op0=ALU.mult, op1=ALU.mult), reads=[bankB[k], szB["pz"], constB], writes=[obrB["pool"]])

        proj_silu(7, "mz")
        wt, wtB = load_w(wbf_in, "in", 0, 8, 6 * 512)
        for ct in range(4):
            k = fm_group(wt, wtB, ct, xT, xTB)
            P.op("dve", lambda e, k=k, ct=ct: e.tensor_scalar(out=mqT[:, ct, :], in0=bank(k), scalar1=128.0 ** -0.5,
                                                              scalar2=None, op0=ALU.mult),
                 reads=[bankB[k]], writes=[mqB])
        proj_silu(5, "dz")

        if j == 0:
            mem_kv()
        wq, wqB = load_w(wbf_in, "in", 0, 8, 2 * 512)

        def q_group(ct):
            k = fm_group(wq, wqB, ct, xT, xTB)
            P.op("dve", lambda e, k=k, ct=ct: e.tensor_scalar(out=QT[:, ct, :], in0=bank(k), scalar1=32.0 ** -0.5,
                                                              scalar2=None, op0=ALU.mult),
                 reads=[bankB[k]], writes=[QTB])

        for hd in range(4):
            kk = (nb() // 2) * 2
            bank_ctr[0] = kk + 2
            pmt, pmB = pm.next()
            for mt in range(2):
                mm_group(bank(kk + mt), [(mkT[:, hd, mt * 128:(mt + 1) * 128], mqT[:, hd, :])], [mkB, mqB],
                         bankB[kk + mt])
            P.op("act", lambda e, kk=kk, pmt=pmt: e.activation(
                out=pmt[:, :, :].rearrange("p a b -> p (a b)"), in_=ps[:, kk * 512:(kk + 2) * 512], func=AF.Exp),
                reads=[bankB[kk], bankB[kk + 1]], writes=[pmB])
            q_group(hd)
            ko = nb()
            mm_group(bank(ko), [(mvb[:, mt, hd * 128:(hd + 1) * 128], pmt[:, mt, :]) for mt in range(2)],
                     [mvB_, pmB], bankB[ko])
            ksum = nb()
            mm_group(bank(ksum), [(ones[:, :], pmt[:, mt, :]) for mt in range(2)], [constB, pmB], bankB[ksum])
            f1, f1B = ftmp.next()
            f2, f2B = ftmp.next()
            P.op("dve", lambda e, ksum=ksum, f1=f1: e.reciprocal(out=f1[:, :], in_=bank(ksum)),
                 reads=[bankB[ksum]], writes=[f1B])
            P.op("dve", lambda e, ko=ko, f1=f1, f2=f2: e.tensor_tensor(out=f2[:, :], in0=bank(ko), in1=f1[:, :], op=ALU.mult),
                 reads=[bankB[ko], f1B], writes=[f2B])
            P.op("dve", lambda e, f2=f2, hd=hd: e.tensor_tensor(out=obr["mem"][:, hd, :], in0=f2[:, :],
                                                                in1=sz["mz"][:, hd, :], op=ALU.mult),
                 reads=[f2B, szB["mz"]], writes=[obrB["mem"]])

        entries = [("own", i, False) for i in range(j)] + [("oth", i, False) for i in range(j + 1)] + [("own", j, True)]
        step = [0]

        def live(st_, si, kt, h, diag):
            if diag:
                return True
            for r in range(2):
                cq = OWN[r][j]
                kc = (OWN[r] if st_ == "own" else OWN[1 - r])[si]
                if kc > cq:
                    continue
                if SLOPES[h] * (512 * (cq - kc) - 128 * kt - 127) <= SKIP_THRESH:
                    return True
            return False

        deferred = [None, 0]
        for hp in range(4):
            first = [True, True]
            pending = [None]
            for ei, (st_, si, diag) in enumerate(entries):
                lv = {(kt, hl): live(st_, si, kt, 2 * hp + hl, diag) for kt in range(4) for hl in range(2)}
                if not any(lv.values()):
                    continue
                kb, kB = kbufs.next()
                vb, vB = vbufs.next()
                P.dma("sp", lambda e, kb=kb, st_=st_, si=si, hp=hp: e.dma_start(
                    out=kb[:, :], in_=KTd[st_][hp, :, si * CH:(si + 1) * CH]), reads=[KTd_B[st_][si]], writes=[kB])
                P.dma("sp", lambda e, vb=vb, st_=st_, si=si, hp=hp: e.dma_start(
                    out=vb[:, :, :], in_=Vd[st_][si * 4:(si + 1) * 4, :, hp * 128:(hp + 1) * 128].rearrange("kt p c -> p kt c")),
                    reads=[Vd_B[st_][si]], writes=[vB])
                kts = (3, 2, 1, 0) if diag else (0, 1, 2, 3)
                for kt in kts:
                    lo = kt * 128 if diag else 0
                    last = diag and kt == 0
                    hls = [hl for hl in range(2) if lv[(kt, hl)]]
                    if not hls:
                        continue
                    pi = step[0] % 3
                    step[0] += 1
                    pt = PTs[pi]
                    for hl in hls:
                        h = 2 * hp + hl
                        sB = [bankB[2 * hl], bankB[2 * hl + 1]]

                        def fsc(e, kb=kb, kt=kt, hl=hl, hp=hp, lo=lo):
                            ins = None
                            for c in range(2):
                                g = 2 * hl + c
                                o = ps[:, g * 512 + lo:(g + 1) * 512]
                                ins = e.matmul(o, lhsT=kb[32 * g:32 * g + 32, kt * 128:(kt + 1) * 128],
                                               rhs=QT[32 * g:32 * g + 32, hp, lo:CH], start=True, stop=(hp >= 1),
                                               tile_position=(32 * g, 0))
                                if hp == 0:
                                    ins = e.matmul(o, lhsT=TK[32 * g:32 * g + 3, hp * 128:(hp + 1) * 128],
                                                   rhs=TQ[32 * g:32 * g + 3, hp * 512 + lo:(hp + 1) * 512],
                                                   start=False, stop=True, tile_position=(32 * g, 0))
                            return ins
                        P.op("pe", fsc, reads=[kB, QTB, constB], writes=sB)
                        bidx = (ei * 4 + kt) * 8 + h
                        P.op("act", lambda e, pt=pt, hl=hl, lo=lo, bidx=bidx, bmt=bmt: e.activation(
                            out=pt[:, hl, :].rearrange("p (c q) -> p c q", c=2)[:, :, lo:CH],
                            in_=ps[:, 2 * hl * 512:(2 * hl + 2) * 512].rearrange("p (c q) -> p c q", c=2)[:, :, lo:CH],
                            func=AF.Exp, bias=bmt[:, bidx:bidx + 1], scale=1.0),
                            reads=sB + [bmB], writes=[PTB[pi][hl]])
                        if diag:
                            P.op("dve", lambda e, pt=pt, hl=hl, lo=lo: e.tensor_tensor(
                                out=pt[:, hl, :].rearrange("p (c q) -> p c q", c=2)[:, :, lo:lo + 128],
                                in0=pt[:, hl, :].rearrange("p (c q) -> p c q", c=2)[:, :, lo:lo + 128],
                                in1=maskb[:, :].unsqueeze(1).to_broadcast([128, 2, 128]), op=ALU.mult),
                                reads=[PTB[pi][hl], constB], writes=[PTB[pi][hl]])

                    def fpv(e, vb=vb, kt=kt, pt=pt, lo=lo, fst=tuple(first), last=last, hls=tuple(hls)):
                        ins = None
                        for c in range(2):
                            for hl in hls:
                                rhs = pt[:, hl, c * 512 + lo:(c + 1) * 512]
                                e.matmul(ps[64 * hl:64 * hl + 64, (4 + c) * 512 + lo:(5 + c) * 512],
                                         lhsT=vb[:, kt, 64 * hl:64 * hl + 64], rhs=rhs, start=fst[hl], stop=last,
                                         tile_position=(0, 64 * hl))
                                ins = e.matmul(ps[64 * hl:64 * hl + 64, (6 + c) * 512 + lo:(7 + c) * 512],
                                               lhsT=ones[:, 0:64], rhs=rhs, start=fst[hl], stop=last,
                                               tile_position=(0, 64 * hl))
                        return ins
                    pv_args = (fpv, [vB, constB] + [PTB[pi][hl] for hl in hls], [bankB[4], bankB[5], bankB[6], bankB[7]])
                    for hl in hls:
                        first[hl] = False
                    if pending[0] is not None:
                        P.op("pe", pending[0][0], reads=pending[0][1], writes=pending[0][2])
                    pending[0] = pv_args
                    if deferred[0] is not None:
                        deferred[1] -= 1
                        if deferred[1] <= 0:
                            deferred[0]()
                            deferred[0] = None
            P.op("pe", pending[0][0], reads=pending[0][1], writes=pending[0][2])
            if deferred[0] is not None:
                deferred[0]()
                deferred[0] = None
            r0, r0B = ftmp.next()
            r1, r1B = ftmp.next()
            a0, a0B = ftmp.next()
            a1, a1B = ftmp.next()
            P.op("dve", lambda e, a0=a0: e.tensor_copy(out=a0[:, :], in_=bank(4)), reads=[bankB[4]], writes=[a0B])
            P.op("dve", lambda e, a1=a1: e.tensor_copy(out=a1[:, :], in_=bank(5)), reads=[bankB[5]], writes=[a1B])
            P.op("dve", lambda e, r0=r0: e.tensor_copy(out=r0[:, :], in_=bank(6)), reads=[bankB[6]], writes=[r0B])
            P.op("dve", lambda e, r1=r1: e.tensor_copy(out=r1[:, :], in_=bank(7)), reads=[bankB[7]], writes=[r1B])
            P.op("dve", lambda e, r0=r0: e.reciprocal(out=r0[:, :], in_=r0[:, :]), reads=[r0B], writes=[r0B])
            P.op("dve", lambda e, r1=r1: e.reciprocal(out=r1[:, :], in_=r1[:, :]), reads=[r1B], writes=[r1B])
            P.op("dve", lambda e, r0=r0, a0=a0: e.tensor_tensor(out=a0[:, :], in0=a0[:, :], in1=r0[:, :], op=ALU.mult),
                 reads=[a0B, r0B], writes=[a0B])
            P.op("dve", lambda e, r1=r1, a1=a1: e.tensor_tensor(out=a1[:, :], in0=a1[:, :], in1=r1[:, :], op=ALU.mult),
                 reads=[a1B, r1B], writes=[a1B])
            P.op("dve", lambda e, a0=a0, a1=a1: e.scalar_tensor_tensor(out=a0[:, :], in0=a1[:, :], scalar=neglam,
                                                                       in1=a0[:, :], op0=ALU.mult, op1=ALU.add),
                 reads=[a0B, a1B, constB], writes=[a0B])
            P.op("dve", lambda e, a0=a0, a1=a1: e.tensor_tensor(out=a1[:, :], in0=a0[:, :], in1=a0[:, :], op=ALU.mult),
                 reads=[a0B], writes=[a1B])

            def epi2(r0=r0, r0B=r0B, a0=a0, a0B=a0B, a1=a1, a1B=a1B, hp=hp):
                mm_group(bank(0), [(blk[:, :], a1[:, :])], [constB, a1B], bankB[0])
                P.op("dve", lambda e: e.tensor_scalar(out=r0[:, :], in0=bank(0), scalar1=64.0 * RMS_EPS, scalar2=None,
                                                      op0=ALU.add), reads=[bankB[0]], writes=[r0B])
                P.op("act", lambda e: e.activation(out=r0[:, :], in_=r0[:, :], func=AF.Sqrt), reads=[r0B], writes=[r0B])
                P.op("dve", lambda e: e.reciprocal(out=r0[:, :], in_=r0[:, :]), reads=[r0B], writes=[r0B])
                P.op("dve", lambda e: e.tensor_tensor(out=a0[:, :], in0=a0[:, :], in1=r0[:, :], op=ALU.mult),
                     reads=[a0B, r0B], writes=[a0B])
                P.op("dve", lambda e: e.scalar_tensor_tensor(
                    out=obr["diff"][:, hp, :], in0=a0[:, :], scalar=gsc[:, 0:1], in1=sz["dz"][:, hp, :],
                    op0=ALU.mult, op1=ALU.mult), reads=[a0B, szB["dz"], constB], writes=[obrB["diff"]])
            if hp < 3:
                deferred[0], deferred[1] = epi2, 6
            else:
                epi2()

        sched = {}
        if j + 1 < RUN_SLOTS:
            jn = j + 1
            tl = [ln_tile(x_oth[jn * CH + tt * 128:jn * CH + (tt + 1) * 128, :], 128, xTo, xToB, tt * 128)
                  for tt in range(4)]
            sched = {0: [tl[0][0]], 1: [tl[1][0]], 2: [tl[0][1], tl[2][0]], 3: [tl[1][1], tl[3][0]],
                     4: [tl[2][1]], 5: [tl[3][1]]}
        for hf in range(2):
            wg = []
            for n in range(3):
                wg.append(load_w(wbf_in, "in", 0, 8, (8 + 2 * n + hf) * 512))
            for dl in range(4):
                dt_ = 4 * hf + dl
                for piece in sched.get(dt_, ()):
                    piece()
                gs = []
                for n in range(3):
                    k = fm_group(wg[n][0], wg[n][1], dl, xT, xTB)
                    gt, gB = gsig.next()
                    P.op("act", lambda e, k=k, gt=gt, n=n, dt_=dt_: e.activation(
                        out=gt[:, :], in_=bank(k), func=AF.Sigmoid, bias=bg[:, n * 8 + dt_:n * 8 + dt_ + 1], scale=1.0),
                        reads=[bankB[k], constB], writes=[gB])
                    gs.append((gt, gB))
                if dl == 0:
                    wb = []
                    for n in range(3):
                        wbt, wbB = wbring.next()
                        wb.append(load_w(wbf_br, "br", n * 512, 4, hf * 512, wbt, wbB))
                ys = []
                for n, key in enumerate(("pool", "diff", "mem")):
                    k = nb()
                    mm_group(bank(k), [(wb[n][0][:, cc, dl * 128:(dl + 1) * 128], obr[key][:, cc, :]) for cc in range(4)],
                             [wb[n][1], obrB[key]], bankB[k])
                    ys.append(k)
                m, mB = ftmp.next()
                t, tB = ftmp.next()
                P.op("dve", lambda e, m=m, k=ys[0], g=gs[0][0]: e.tensor_tensor(out=m[:, :], in0=bank(k), in1=g[:, :], op=ALU.mult),
                     reads=[bankB[ys[0]], gs[0][1]], writes=[mB])
                P.op("dve", lambda e, t=t, k=ys[1], g=gs[1][0]: e.tensor_tensor(out=t[:, :], in0=bank(k), in1=g[:, :], op=ALU.mult),
                     reads=[bankB[ys[1]], gs[1][1]], writes=[tB])
                P.op("pool", lambda e, m=m, t=t: e.tensor_tensor(out=m[:, :], in0=m[:, :], in1=t[:, :], op=ALU.add),
                     reads=[mB, tB], writes=[mB])
                t2, t2B = ftmp.next()
                P.op("dve", lambda e, t2=t2, k=ys[2], g=gs[2][0]: e.tensor_tensor(out=t2[:, :], in0=bank(k), in1=g[:, :], op=ALU.mult),
                     reads=[bankB[ys[2]], gs[2][1]], writes=[t2B])
                P.op("pool", lambda e, m=m, t2=t2, dt_=dt_: e.tensor_tensor(out=merged[:, dt_, :], in0=m[:, :], in1=t2[:, :], op=ALU.add),
                     reads=[mB, t2B], writes=[mergedB])
        if j + 1 < RUN_SLOTS:
            kv_project(xTo, xToB, "oth", j + 1)
        wo = [load_w(wbf_out, "out", 0, 8, hf2 * 512) for hf2 in range(2)]
        def final_tile(tt, j=j):
            rb, rbB = rbuf.next()
            st, stB = stt.next()
            mv, mvB2 = mvr.next()

            def fa():
                for hf2 in range(2):
                    k = nb()
                    mm_group(bank(k), [(merged[:, kc, tt * 128:(tt + 1) * 128], wo[hf2][0][:, kc, :]) for kc in range(8)],
                             [mergedB, wo[hf2][1]], bankB[k])
                    P.op("dve", lambda e, k=k, hf2=hf2: e.tensor_tensor(
                        out=rb[:, hf2 * 512:(hf2 + 1) * 512], in0=bank(k), in1=res[:, tt, hf2 * 512:(hf2 + 1) * 512],
                        op=ALU.add), reads=[bankB[k], resB[tt]], writes=[rbB])
                P.op("dve", lambda e: e.bn_stats(out=st[:, 0:6], in_=rb[:, 0:512]), reads=[rbB], writes=[stB])
                P.op("dve", lambda e: e.bn_stats(out=st[:, 6:12], in_=rb[:, 512:1024]), reads=[rbB], writes=[stB])
                P.op("dve", lambda e: e.bn_aggr(out=mv[:, 0:2], in_=st[:, 0:12]), reads=[stB], writes=[mvB2])
                P.op("dve", lambda e: e.tensor_scalar(out=mv[:, 2:3], in0=mv[:, 1:2], scalar1=LN_EPS, scalar2=None,
                                                      op0=ALU.add), reads=[mvB2], writes=[mvB2])
                P.op("act", lambda e: e.activation(out=mv[:, 2:3], in_=mv[:, 2:3], func=AF.Sqrt),
                     reads=[mvB2], writes=[mvB2])
                P.op("dve", lambda e: e.reciprocal(out=mv[:, 2:3], in_=mv[:, 2:3]), reads=[mvB2], writes=[mvB2])
                P.op("dve", lambda e: e.scalar_tensor_tensor(out=mv[:, 3:4], in0=mv[:, 0:1], scalar=-1.0, in1=mv[:, 2:3],
                                                             op0=ALU.mult, op1=ALU.mult), reads=[mvB2], writes=[mvB2])
                P.op("act", lambda e: e.activation(out=rb[:, :], in_=rb[:, :], func=AF.Identity,
                                                   bias=mv[:, 3:4], scale=mv[:, 2:3]),
                     reads=[rbB, mvB2], writes=[rbB])

            def fb():
                P.op("dve", lambda e: e.tensor_tensor(out=rb[:, :], in0=rb[:, :], in1=gout[:, :], op=ALU.mult),
                     reads=[rbB, constB], writes=[rbB])
                P.op("pool", lambda e: e.tensor_tensor(out=rb[:, :], in0=rb[:, :], in1=bout[:, :], op=ALU.add),
                     reads=[rbB, constB], writes=[rbB])
                P.dma("pool", lambda e: e.dma_start(
                    out=out_own[j * CH + tt * 128:j * CH + (tt + 1) * 128, :], in_=rb[:, :]), reads=[rbB])
            return fa, fb
        skew([final_tile(tt) for tt in range(4)])

    Epool = engs["pool"]
    for q in dsems:
        for s in dsems[q]:
            if s["total"] > 0 and Epool.waited.get(s["key"], 0) < s["total"]:
                Epool.ops.append(("wait", s["sem"], s["total"]))

    with nc.Block() as block:
        @block.tensor
        def _(e):
            _run(e, engs["pe"])

        @block.scalar
        def _(e):
            _run(e, engs["act"])

        @block.vector
        def _(e):
            _run(e, engs["dve"])

        @block.gpsimd
        def _(e):
            _run(e, engs["pool"])

        @block.sync
        def _(e):
            _run(e, engs["sp"])

    es.close()
    return nc


def _host_tables():
    f = np.float32
    ident = np.eye(128, dtype=f)
    blk = np.zeros((128, 128), f)
    blk[:64, :64] = 1.0
    blk[64:, 64:] = 1.0
    kk = np.arange(128)[:, None]
    qq = np.arange(128)[None, :]
    mask = (kk <= qq).astype(f)
    slopes = np.array([2.0 ** (-(h + 1)) for h in range(8)], dtype=np.float64)
    tk = np.zeros((128, 4, 128), f)
    tq = np.zeros((128, 4, 512), f)
    qi = np.arange(512)
    for hp in range(4):
        for g in range(4):
            h = 2 * hp + g // 2
            tk[32 * g + 0, hp, :] = slopes[h] * np.arange(128)
            tk[32 * g + 1, hp, :] = 1.0
            tk[32 * g + 2, hp, :] = 1.0
            tq[32 * g + 0, hp, :] = 1.0
            tq[32 * g + 1, hp, :] = -slopes[h] * 128.0 * (qi // 128)
            tq[32 * g + 2, hp, :] = -slopes[h] * (qi % 128)
    return ident, blk, mask, tk.reshape(128, 512), tq.reshape(128, 2048), slopes


def _core_tables(r, slopes):
    f = np.float32
    own = OWN[r]
    oth = OWN[1 - r]
    bm = np.zeros((NSLOT, 128, 16, 4, 8), f)
    hv = np.zeros((128, 8), f)
    invc = np.zeros((128, 8, 4, 16), f)
    for j in range(NSLOT):
        c = own[j]
        entries = [own[i] for i in range(j)] + [oth[i] for i in range(j + 1)] + [c]
        for ei, kc in enumerate(entries):
            for kt in range(4):
                if ei == len(entries) - 1:
                    m = kt
                    valid = True
                else:
                    m = (512 * kc + 128 * kt - 512 * c) // 128
                    valid = kc < c
                for h in range(8):
                    if not valid:
                        bm[j, :, ei, kt, h] = NEG_BIG
                    elif h < 2:
                        bm[j, :, ei, kt, h] = slopes[h] * 128.0 * m
                    else:
                        bm[j, :, ei, kt, h] = slopes[h] * (np.arange(128) + 128.0 * m - 256.0)
        hv[:, j] = 0.0 if c == 0 else 1.0
        for g, w in enumerate((2, 4, 8, 16)):
            pos = 512 * c + np.arange(16)
            invc[:, j, g, :] = (1.0 / np.minimum(pos + 1, w)).astype(f)[None, :]
    return bm.reshape(NSLOT, 128, 512), hv, invc.reshape(128, 512)


_NC_CACHE = {}


def kernel(x, mem, ln_in_g, ln_in_b, w_in, b_gate, pool_w, pool_scale, lambda_q1, lambda_k1,
           lambda_q2, lambda_k2, diff_norm_g, w_mem_kv, w_branch, w_out, ln_out_g, ln_out_b):
    f = np.float32
    x = np.asarray(x, f)
    mem = np.asarray(mem, f)
    ident, blk, mask, tk, tq, slopes = _host_tables()
    rep = lambda v: np.ascontiguousarray(np.broadcast_to(np.asarray(v, f).reshape(1, -1), (128, np.asarray(v).size)))
    lam4 = np.concatenate([rep(lambda_q1), rep(lambda_k1), rep(lambda_q2), rep(lambda_k2)], axis=1)
    common = {
        "w_in": np.ascontiguousarray(np.asarray(w_in, f)[0]),
        "w_kv": np.ascontiguousarray(np.asarray(w_mem_kv, f)[0]),
        "w_br": np.ascontiguousarray(np.asarray(w_branch, f)[0].reshape(1536, D)),
        "w_out": np.ascontiguousarray(np.asarray(w_out, f)[0]),
        "poolw": np.ascontiguousarray(np.asarray(pool_w, f)[0].transpose(1, 0, 2).reshape(128, 512)),
        "g_in": rep(ln_in_g), "b_in": rep(ln_in_b),
        "g_fm": np.ascontiguousarray(np.asarray(ln_in_g, f).reshape(8, 128).T),
        "b_fm": np.ascontiguousarray(np.asarray(ln_in_b, f).reshape(8, 128).T),
        "g_out": rep(np.asarray(ln_out_g)[0]), "b_out": rep(np.asarray(ln_out_b)[0]),
        "bgate": np.ascontiguousarray(np.asarray(b_gate, f)[0].reshape(24, 128).T),
        "pscale": np.ascontiguousarray(np.asarray(pool_scale, f)[0].reshape(4, 128).T),
        "dng": np.ascontiguousarray(np.tile(np.asarray(diff_norm_g, f)[0], 2).reshape(128, 1)),
        "lam4": np.ascontiguousarray(lam4),
        "ident": ident, "blk": blk, "mask": mask, "tk": tk, "tq": tq,
    }
    in_maps = []
    for c in range(NCORES):
        b, r = c // 2, c % 2
        own, oth = OWN[r], OWN[1 - r]
        xb = x[b].reshape(16, CH, D)
        halo = np.zeros((NSLOT, HALO, D), f)
        for j, cj in enumerate(own):
            if cj > 0:
                halo[j] = x[b, cj * CH - HALO:cj * CH, :]
        bm, hv, invc = _core_tables(r, slopes)
        m = dict(common)
        m["x_own"] = np.ascontiguousarray(xb[own].reshape(NSLOT * CH, D))
        m["x_oth"] = np.ascontiguousarray(xb[oth].reshape(NSLOT * CH, D))
        m["x_halo"] = halo.reshape(NSLOT * HALO, D)
        m["memT"] = np.ascontiguousarray(mem[b].T)
        m["bm"] = bm
        m["hv"] = hv
        m["invc"] = invc
        in_maps.append(m)
    if "nc" not in _NC_CACHE:
        _NC_CACHE["nc"] = build()
    nc = _NC_CACHE["nc"]
    resu = run_bass_kernel_spmd(nc, in_maps, core_ids=list(range(NCORES)))
    out = np.zeros((BATCH, 16, CH, D), f)
    for c in range(NCORES):
        b, r = c // 2, c % 2
        o = np.asarray(resu.results[c]["out_own"], f).reshape(NSLOT, CH, D)
        for j, cj in enumerate(OWN[r]):
            out[b, cj] = o[j]
    return out.reshape(BATCH, SEQ, D)
```

```python
import contextlib
import numpy as np
import concourse.bass as bass
import concourse.mybir as mybir
from concourse.bass_utils import run_bass_kernel_spmd

F32 = mybir.dt.float32
BF16 = mybir.dt.bfloat16
AF = mybir.ActivationFunctionType
ALU = mybir.AluOpType
AXX = mybir.AxisListType.X

D = 1024
SEQ = 8192
BATCH = 4
NCORES = 8
CH = 512
NSLOT = 8
HALO = 128
OWN = ([0, 3, 4, 7, 8, 11, 12, 15], [1, 2, 5, 6, 9, 10, 13, 14])
ALPHA = 2.0 ** 0.25
LAM_INIT = 0.8 - 0.6 * 1.0
LN_EPS = 1e-5
RMS_EPS = 1e-5
NEG_BIG = -30000.0
ND = 24
SLOPES = [2.0 ** (-(h + 1)) for h in range(8)]
SKIP_THRESH = 144.0

RUN_SLOTS = NSLOT


class Buf:
    __slots__ = ("w", "r")

    def __init__(self):
        self.w = None
        self.r = {}


class MultiBuf:
    def __init__(self):
        self.bufs = []

    def new(self):
        b = Buf()
        self.bufs.append(b)
        return b


def _flat(bufs):
    out = []
    for b in bufs:
        if isinstance(b, MultiBuf):
            out.extend(b.bufs)
        else:
            out.append(b)
    return out


class Eng:
    def __init__(self, name, sem):
        self.name = name
        self.sem = sem
        self.cnt = 0
        self.ops = []
        self.waited = {}


class Prog:
    def __init__(self, engs, dsems):
        self.eng = engs
        self.dsems = dsems
        self.di = {q: 0 for q in dsems}

    def _deps(self, reads, writes):
        deps = []
        for b in reads:
            if b.w is not None:
                deps.append((b.w, True))
        for b in writes:
            if b.w is not None:
                deps.append((b.w, False))
            for ev in b.r.values():
                deps.append((ev, False))
        return deps

    def _emit_waits(self, E, deps, is_dma):
        for (ev, raw) in deps:
            key, sem, val = ev
            if (not is_dma) and key == E.name:
                if E.name == "pe":
                    continue
            if E.waited.get(key, 0) >= val:
                continue
            E.waited[key] = val
            E.ops.append(("wait", sem, val))

    def op(self, en, fn, reads=(), writes=()):
        reads, writes = _flat(reads), _flat(writes)
        E = self.eng[en]
        self._emit_waits(E, self._deps(reads, writes), False)
        E.cnt += 1
        ev = (en, E.sem, E.cnt)
        E.ops.append(("op", fn))
        for b in reads:
            b.r[en] = ev
        for b in writes:
            b.w = ev
            b.r = {}

    def dma(self, qn, fn, reads=(), writes=()):
        reads, writes = _flat(reads), _flat(writes)
        Q = self.eng[qn]
        pool = self.dsems[qn]
        s = pool[self.di[qn] % len(pool)]
        self.di[qn] += 1
        deps = self._deps(reads, writes)
        if s["total"] > 0:
            deps.append(((s["key"], s["sem"], s["total"]), True))
        self._emit_waits(Q, deps, True)
        s["total"] += 16
        ev = (s["key"], s["sem"], s["total"])
        Q.ops.append(("dma", fn, s["sem"]))
        for b in reads:
            b.r[s["key"]] = ev
        for b in writes:
            b.w = ev
            b.r = {}


def _run(e, E):
    for o in E.ops:
        if o[0] == "wait":
            e.wait_ge(o[1], o[2])
        elif o[0] == "op":
            o[1](e).then_inc(E.sem, 1)
        else:
            o[1](e).then_inc(o[2], 16)


class Ring:
    def __init__(self, tiles):
        self.tiles = tiles
        self.bufs = [Buf() for _ in tiles]
        self.i = 0

    def next(self):
        k = self.i % len(self.tiles)
        self.i += 1
        return self.tiles[k], self.bufs[k]


def build():
    nc = bass.Bass("TRN2", target_bir_lowering=False)
    es = contextlib.ExitStack()

    def din(name, shape, dt=F32):
        return nc.dram_tensor(name, list(shape), dt, kind="ExternalInput").ap()

    x_own = din("x_own", [NSLOT * CH, D])
    x_oth = din("x_oth", [NSLOT * CH, D])
    x_halo = din("x_halo", [NSLOT * HALO, D])
    memT = din("memT", [D, 256])
    w_in = din("w_in", [D, 7168])
    w_kv = din("w_kv", [D, 1024])
    w_br = din("w_br", [1536, D])
    w_out = din("w_out", [D, D])
    poolw = din("poolw", [128, 512])
    g_in = din("g_in", [128, D])
    b_in = din("b_in", [128, D])
    g_out = din("g_out", [128, D])
    b_out = din("b_out", [128, D])
    g_fm = din("g_fm", [128, 8])
    b_fm = din("b_fm", [128, 8])
    bgate = din("bgate", [128, 24])
    pscale = din("pscale", [128, 4])
    dng = din("dng", [128, 1])
    lam4 = din("lam4", [128, 128])
    ident_d = din("ident", [128, 128])
    blk_d = din("blk", [128, 128])
    mask_d = din("mask", [128, 128])
    tk_d = din("tk", [128, 512])
    tq_d = din("tq", [128, 2048])
    bm_d = din("bm", [NSLOT, 128, 512])
    hv_d = din("hv", [128, 8])
    invc_d = din("invc", [128, 512])
    out_own = nc.dram_tensor("out_own", [NSLOT * CH, D], F32, kind="ExternalOutput").ap()

    wbf_in = nc.dram_tensor("wbf_in", [D, 7168], BF16).ap()
    wbf_kv = nc.dram_tensor("wbf_kv", [D, 1024], BF16).ap()
    wbf_br = nc.dram_tensor("wbf_br", [1536, D], BF16).ap()
    wbf_out = nc.dram_tensor("wbf_out", [D, D], BF16).ap()
    KTd = {s: nc.dram_tensor("ktd_" + s, [4, 128, NSLOT * CH], BF16).ap() for s in ("own", "oth")}
    Vd = {s: nc.dram_tensor("vd_" + s, [NSLOT * 4, 128, 512], BF16).ap() for s in ("own", "oth")}
    KTd_B = {s: [Buf() for _ in range(NSLOT)] for s in ("own", "oth")}
    Vd_B = {s: [Buf() for _ in range(NSLOT)] for s in ("own", "oth")}
    WB = {k: Buf() for k in ("in", "kv", "br", "out")}

    def sb(name, shape, dt):
        return es.enter_context(nc.sbuf_tensor("sb_" + name, list(shape), dt))

    def sem(name):
        return es.enter_context(nc.semaphore(name))

    engs = {n: Eng(n, sem("s_" + n)) for n in ("pe", "act", "dve", "pool", "sp")}
    dsems = {q: [dict(key=("d" + q, i), sem=sem("d%s%d" % (q, i)), total=0) for i in range(n)]
             for q, n in (("sp", 16), ("pool", 12))}
    P = Prog(engs, dsems)

    ps = es.enter_context(nc.psum_tensor("ps", [128, 4096], F32))
    bankB = [Buf() for _ in range(8)]
    bank_ctr = [0]

    def nb():
        k = bank_ctr[0] % 8
        bank_ctr[0] += 1
        return k

    def bank(k, n=512, p0=0, p1=128):
        return ps[p0:p1, k * 512:k * 512 + n]

    def bank_bf(k):
        return ps[:, k * 512:(k + 1) * 512].bitcast(BF16)

    ident = sb("ident", [128, 128], BF16)
    blk = sb("blk", [128, 128], F32)
    maskb = sb("maskb", [128, 128], BF16)
    ones = sb("ones", [128, 128], BF16)
    TK = sb("TK", [128, 128], BF16)
    TQ = sb("TQ", [128, 512], BF16)
    gin = sb("gin", [128, D], F32)
    bin_ = sb("bin", [128, D], F32)
    gout = sb("gout", [128, D], F32)
    bout = sb("bout", [128, D], F32)
    bg = sb("bg", [128, 24], F32)
    gfm = sb("gfm", [128, 8], F32)
    bfm = sb("bfm", [128, 8], F32)
    psc = sb("psc", [128, 4], F32)
    gsc = sb("gsc", [128, 1], F32)
    lamt = sb("lamt", [128, 128], F32)
    lamw = sb("lamw", [128, 8], F32)
    hv = sb("hv", [128, 8], F32)
    invc = sb("invc", [128, 512], F32)
    pwb = sb("pwb", [128, 512], BF16)
    mkT = sb("mkT", [128, 4, 256], BF16)
    mvb = sb("mvb", [128, 2, 512], BF16)
    constB = MultiBuf()
    constB.new()
    mkB, mvB_ = Buf(), Buf()

    NW = 3
    wring = Ring([sb("wr%d" % i, [128, 8, 512], BF16) for i in range(NW)])
    wbring = Ring([sb("wbr%d" % i, [128, 4, 512], BF16) for i in range(3)])
    xin = Ring([sb("xin%d" % i, [128, D], F32) for i in range(2)])
    xhb = Ring([sb("xhb%d" % i, [128, D], BF16) for i in range(2)])
    stt = Ring([sb("stt%d" % i, [128, 12], F32) for i in range(2)])
    mvr = Ring([sb("mvr%d" % i, [128, 4], F32) for i in range(2)])

    xT = sb("xT", [128, 8, CH], BF16)
    xTB = Buf()
    xTh = sb("xTh", [128, 8, HALO], BF16)
    xThB = Buf()
    res = sb("res", [128, 4, D], F32)
    stage = res[:, 0:2, :].rearrange("p a b -> p (a b)")
    resB = [Buf() for _ in range(4)]
    QT = sb("QT", [128, 4, CH], BF16)
    QTB = Buf()
    mqT = sb("mqT", [128, 4, CH], BF16)
    mqB = Buf()
    sz = {k: sb("sz_" + k, [128, 4, CH], BF16) for k in ("pz", "dz", "mz")}
    szB = {k: Buf() for k in ("pz", "dz", "mz")}
    uT = sb("uT", [128, 4, HALO + CH], F32)
    uTB = [Buf() for _ in range(4)]
    pa = sb("pa", [128, HALO + CH], F32)
    pb = sb("pb", [128, HALO + CH], F32)
    paB, pbB = Buf(), Buf()
    pooled = Ring([sb("pooled%d" % i, [128, CH], BF16) for i in range(2)])
    obr = {k: sb("o_" + k, [128, 4, CH], BF16) for k in ("pool", "diff", "mem")}
    obrB = {k: Buf() for k in ("pool", "diff", "mem")}
    kst = Ring([sb("kst%d" % i, [128, CH], BF16) for i in range(2)])
    pm = Ring([sb("pm%d" % i, [128, 2, CH], BF16) for i in range(2)])
    ftmp = Ring([sb("ftmp%d" % i, [128, CH], F32) for i in range(4)])
    gsig = Ring([sb("gsig%d" % i, [128, CH], BF16) for i in range(3)])
    merged = sb("merged", [128, 8, CH], BF16)
    mergedB = Buf()
    rbuf = Ring([sb("rbuf%d" % i, [128, D], F32) for i in range(3)])
    kbufs = Ring([sb("kbuf%d" % i, [128, CH], BF16) for i in range(2)])
    vbufs = Ring([sb("vbuf%d" % i, [128, 4, 128], BF16) for i in range(2)])
    PTall = sb("ptall", [128, 3, 2048], BF16)
    PTs = [PTall[:, i, :].rearrange("p (a b) -> p a b", a=2) for i in range(3)]
    PTB = [[Buf(), Buf()] for _ in range(3)]
    xTo = PTall[:, 0:2, :].rearrange("p a b -> p (a b)").rearrange("p (kc t) -> p kc t", kc=8)
    xToB = MultiBuf()
    xToB.new()
    xToB.bufs.extend([PTB[0][0], PTB[0][1], PTB[1][0], PTB[1][1]])
    bmr = Ring([sb("bmr%d" % i, [128, 512], F32) for i in range(2)])

    def load_const(dst, src, cast_via=None):
        cb = constB.new()
        if cast_via is None:
            P.dma("sp", lambda e, d=dst, s=src: e.dma_start(out=d, in_=s), writes=[cb])
        else:
            sb_ = Buf()
            P.dma("sp", lambda e, d=cast_via, s=src: e.dma_start(out=d, in_=s), writes=[sb_])
            P.op("dve", lambda e, d=dst, s=cast_via: e.tensor_copy(out=d, in_=s),
                 reads=[sb_], writes=[cb, resB[0], resB[1]])

    WBg = {}

    def cast_w(dst, src, key, r0, nrows, c0, ncols):
        b = Buf()
        for c in range(c0, c0 + ncols, 512):
            WBg[(key, r0, c)] = b
        P.dma("pool", lambda e, d=dst[r0:r0 + nrows, c0:c0 + ncols], s=src[r0:r0 + nrows, c0:c0 + ncols]:
              e.dma_start(out=d, in_=s), writes=[b])

    cast_w(wbf_in, w_in, "in", 0, D, 1536, 1024)
    cast_w(wbf_in, w_in, "in", 0, D, 0, 1536)
    cast_w(wbf_in, w_in, "in", 0, D, 2560, 1536)
    cast_w(wbf_kv, w_kv, "kv", 0, D, 0, 1024)
    cast_w(wbf_in, w_in, "in", 0, D, 4096, 3072)
    for n in range(3):
        cast_w(wbf_br, w_br, "br", n * 512, 512, 0, 1024)
    cast_w(wbf_out, w_out, "out", 0, D, 0, 1024)

    def cast_batch1():
        pass

    def cast_batch2():
        pass

    load_const(ident[:, :], ident_d, stage[:, 0:128])
    load_const(maskb[:, :], mask_d, stage[:, 128:256])
    load_const(TK[:, :], tk_d[:, 0:128], stage[:, 256:384])
    load_const(pwb[:, :], poolw, stage[:, 768:1280])
    load_const(blk[:, :], blk_d)
    load_const(gin[:, :], g_in)
    load_const(bin_[:, :], b_in)
    load_const(gout[:, :], g_out)
    load_const(bout[:, :], b_out)
    load_const(bg[:, :], bgate)
    load_const(gfm[:, :], g_fm)
    load_const(bfm[:, :], b_fm)
    load_const(psc[:, :], pscale)
    load_const(gsc[:, :], dng)
    load_const(lamt[:, :], lam4)
    load_const(hv[:, :], hv_d)
    load_const(invc[:, :], invc_d)
    load_const(TQ[:, :], tq_d[:, 0:512], stage[:, 1280:1792])
    P.op("dve", lambda e: e.memset(ones[:, :], 1.0), writes=[constB])
    P.op("dve", lambda e: e.tensor_scalar(out=gin[:, :], in0=gin[:, :], scalar1=ALPHA, scalar2=None, op0=ALU.mult),
         reads=[constB], writes=[constB])
    P.op("dve", lambda e: e.tensor_scalar(out=bin_[:, :], in0=bin_[:, :], scalar1=ALPHA, scalar2=None, op0=ALU.mult),
         reads=[constB], writes=[constB])
    P.op("dve", lambda e: e.tensor_scalar(out=gsc[:, :], in0=gsc[:, :], scalar1=(1.0 - LAM_INIT) * 8.0, scalar2=None,
                                          op0=ALU.mult), reads=[constB], writes=[constB])
    P.op("dve", lambda e: e.tensor_tensor(out=lamt[:, 0:32], in0=lamt[:, 0:32], in1=lamt[:, 32:64], op=ALU.mult),
         reads=[constB], writes=[constB])
    P.op("dve", lambda e: e.tensor_tensor(out=lamt[:, 64:96], in0=lamt[:, 64:96], in1=lamt[:, 96:128], op=ALU.mult),
         reads=[constB], writes=[constB])
    P.op("dve", lambda e: e.reduce_sum(out=lamw[:, 0:1], in_=lamt[:, 0:32], axis=AXX), reads=[constB], writes=[constB])
    P.op("dve", lambda e: e.reduce_sum(out=lamw[:, 1:2], in_=lamt[:, 64:96], axis=AXX), reads=[constB], writes=[constB])
    P.op("act", lambda e: e.activation(out=lamw[:, 2:4], in_=lamw[:, 0:2], func=AF.Exp), reads=[constB], writes=[constB])
    P.op("dve", lambda e: e.tensor_tensor(out=lamw[:, 4:5], in0=lamw[:, 3:4], in1=lamw[:, 2:3], op=ALU.subtract),
         reads=[constB], writes=[constB])
    P.op("dve", lambda e: e.tensor_scalar(out=lamw[:, 5:6], in0=lamw[:, 4:5], scalar1=-LAM_INIT, scalar2=None,
                                          op0=ALU.add), reads=[constB], writes=[constB])
    neglam = lamw[:, 5:6]

    def wsrc(w, r0, nkc, c0):
        return w[r0:r0 + nkc * 128, c0:c0 + 512].rearrange("(kc p) c -> p kc c", p=128)

    def load_w(w, key, r0, nkc, c0, dst=None, dstB=None):
        if dst is None:
            dst, dstB = wring.next()
        P.dma("sp", lambda e, d=dst[:, 0:nkc, :], s=wsrc(w, r0, nkc, c0): e.dma_start(out=d, in_=s),
              reads=[WBg[(key, r0, c0)]], writes=[dstB])
        return dst, dstB


    def mm_group(out_ap, pairs, bufs_r, bufB, tp=None):
        def f(e, out_ap=out_ap, pairs=pairs):
            n = len(pairs)
            ins = None
            for i, (l, r) in enumerate(pairs):
                ins = e.matmul(out_ap, lhsT=l, rhs=r, start=(i == 0), stop=(i == n - 1))
            return ins
        P.op("pe", f, reads=bufs_r, writes=[bufB])

    def mem_kv():
        halves = []
        for hh in range(2):
            stg, stgB = rbuf.next()
            P.dma("sp", lambda e, stg=stg, hh=hh: e.dma_start(
                out=stg[:, :].rearrange("p (kc m) -> p kc m", kc=4),
                in_=memT[hh * 512:(hh + 1) * 512, :].rearrange("(kc p) m -> p kc m", p=128)), writes=[stgB])
            halves.append((stg, stgB))
        for hh, (stg, stgB) in enumerate(halves):
            P.op("dve", lambda e, stg=stg, hh=hh: e.tensor_copy(
                out=xTo[:, 4 * hh:4 * hh + 4, 0:256], in_=stg[:, :].rearrange("p (kc m) -> p kc m", kc=4)),
                reads=[stgB], writes=[xToB])
        wmk, wmkB = load_w(wbf_kv, "kv", 0, 8, 0)
        wmv, wmvB = load_w(wbf_kv, "kv", 0, 8, 512)
        for hd in range(4):
            k = nb()
            mm_group(bank(k, 256), [(wmk[:, kc, hd * 128:(hd + 1) * 128], xTo[:, kc, 0:256]) for kc in range(8)],
                     [wmkB, xToB], bankB[k])
            P.op("dve", lambda e, k=k, hd=hd: e.tensor_copy(out=mkT[:, hd, :], in_=bank(k, 256)),
                 reads=[bankB[k]], writes=[mkB])
        for mt in range(2):
            k = nb()
            mm_group(bank(k), [(xTo[:, kc, mt * 128:(mt + 1) * 128], wmv[:, kc, :]) for kc in range(8)],
                     [wmvB, xToB], bankB[k])
            P.op("dve", lambda e, k=k, mt=mt: e.tensor_copy(out=mvb[:, mt, :], in_=bank(k)),
                 reads=[bankB[k]], writes=[mvB_])

    def skew(stages):
        prev = None
        for pa_, pb_ in stages:
            pa_()
            if prev is not None:
                prev()
            prev = pb_
        if prev is not None:
            prev()

    def ln_tile(src_rows, nrows, dstT, dstTB, col0, keep=None):
        xt, xB = xin.next()
        st, stB = stt.next()
        mv, mvB2 = mvr.next()
        k = nb()
        bt = bank_bf(k)
        hb, hbB = xhb.next()
        btv = bt.rearrange("p (kc t) -> p kc t", kc=8)[:, :, 0:nrows]

        def tr(e):
            ins = None
            for kc in range(8):
                ins = e.transpose(out=bt[:, kc * 128:kc * 128 + nrows], in_=hb[0:nrows, kc * 128:(kc + 1) * 128],
                                  identity=ident[0:nrows, 0:nrows])
            return ins

        def stage_a():
            P.dma("sp", lambda e: e.dma_start(out=xt[0:nrows, :], in_=src_rows), writes=[xB])
            P.op("dve", lambda e: e.bn_stats(out=st[0:nrows, 0:6], in_=xt[0:nrows, 0:512]), reads=[xB], writes=[stB])
            P.op("dve", lambda e: e.bn_stats(out=st[0:nrows, 6:12], in_=xt[0:nrows, 512:1024]), reads=[xB], writes=[stB])
            P.op("dve", lambda e: e.bn_aggr(out=mv[0:nrows, 0:2], in_=st[0:nrows, 0:12]), reads=[stB], writes=[mvB2])
            P.op("dve", lambda e: e.tensor_scalar(out=mv[0:nrows, 2:3], in0=mv[0:nrows, 1:2], scalar1=LN_EPS,
                                                  scalar2=None, op0=ALU.add), reads=[mvB2], writes=[mvB2])
            P.op("act", lambda e: e.activation(out=mv[0:nrows, 2:3], in_=mv[0:nrows, 2:3], func=AF.Sqrt),
                 reads=[mvB2], writes=[mvB2])
            P.op("dve", lambda e: e.reciprocal(out=mv[0:nrows, 2:3], in_=mv[0:nrows, 2:3]), reads=[mvB2], writes=[mvB2])
            P.op("dve", lambda e: e.scalar_tensor_tensor(out=mv[0:nrows, 3:4], in0=mv[0:nrows, 0:1], scalar=-1.0,
                                                         in1=mv[0:nrows, 2:3], op0=ALU.mult, op1=ALU.mult),
                 reads=[mvB2], writes=[mvB2])
            if keep is None:
                P.op("act", lambda e: e.activation(out=hb[0:nrows, :], in_=xt[0:nrows, :], func=AF.Identity,
                                                   bias=mv[0:nrows, 3:4], scale=mv[0:nrows, 2:3]),
                     reads=[xB, mvB2], writes=[hbB])
            else:
                rt, rB = keep
                P.op("act", lambda e: e.activation(out=rt, in_=xt[0:nrows, :], func=AF.Identity, bias=mv[0:nrows, 3:4],
                                                   scale=mv[0:nrows, 2:3]), reads=[xB, mvB2], writes=[rB])

        def stage_b():
            if keep is None:
                P.op("pe", tr, reads=[hbB, constB], writes=[bankB[k]])
                tmp, tmpB = rbuf.next()
                tv = tmp[:, :].rearrange("p (kc t) -> p kc t", kc=8)[:, :, 0:nrows]
                P.op("dve", lambda e: e.tensor_tensor(out=tv, in0=btv,
                                                      in1=gfm[:, :].unsqueeze(2).to_broadcast([128, 8, nrows]),
                                                      op=ALU.mult), reads=[bankB[k], constB], writes=[tmpB])
                P.op("pool", lambda e: e.tensor_tensor(out=dstT[:, :, col0:col0 + nrows], in0=tv,
                                                       in1=bfm[:, :].unsqueeze(2).to_broadcast([128, 8, nrows]),
                                                       op=ALU.add), reads=[tmpB, constB], writes=[dstTB])
            else:
                rt, rB = keep
                P.op("dve", lambda e: e.tensor_tensor(out=rt, in0=rt, in1=gin[0:nrows, :], op=ALU.mult),
                     reads=[rB, constB], writes=[rB])
                P.op("pool", lambda e: e.tensor_tensor(out=rt, in0=rt, in1=bin_[0:nrows, :], op=ALU.add),
                     reads=[rB, constB], writes=[rB])
                P.op("dve", lambda e: e.tensor_scalar(out=hb[0:nrows, :], in0=rt, scalar1=1.0 / ALPHA, scalar2=None,
                                                      op0=ALU.mult), reads=[rB], writes=[hbB])
                P.op("pe", tr, reads=[hbB, constB], writes=[bankB[k]])
                P.op("act", lambda e: e.activation(out=dstT[:, :, col0:col0 + nrows], in_=btv, func=AF.Copy),
                     reads=[bankB[k]], writes=[dstTB])
        return stage_a, stage_b

    def kv_project(src_xT, src_B, store, j, part=None):
        if part in (None, "k"):
            wk, wkB = load_w(wbf_in, "in", 0, 8, 3 * 512)
        if part in (None, "v"):
            wv, wvB = load_w(wbf_in, "in", 0, 8, 4 * 512)
        for hp in (range(4) if part in (None, "k") else ()):
            k = nb()
            mm_group(bank(k), [(wk[:, kc, hp * 128:(hp + 1) * 128], src_xT[:, kc, :]) for kc in range(8)],
                     [wkB, src_B], bankB[k])
            ks, ksB = kst.next()
            P.op("act", lambda e, k=k, ks=ks: e.activation(out=ks[:, :], in_=bank(k), func=AF.Copy),
                 reads=[bankB[k]], writes=[ksB])
            P.dma("pool", lambda e, ks=ks, hp=hp: e.dma_start(out=KTd[store][hp, :, j * CH:(j + 1) * CH], in_=ks[:, :]),
                  reads=[ksB], writes=[KTd_B[store][j]])
        for tt in (range(4) if part in (None, "v") else ()):
            k = nb()
            mm_group(bank(k), [(src_xT[:, kc, tt * 128:(tt + 1) * 128], wv[:, kc, :]) for kc in range(8)],
                     [wvB, src_B], bankB[k])
            ks, ksB = kst.next()
            P.op("dve", lambda e, k=k, ks=ks: e.tensor_copy(out=ks[:, :], in_=bank(k)),
                 reads=[bankB[k]], writes=[ksB])
            P.dma("pool", lambda e, ks=ks, tt=tt: e.dma_start(out=Vd[store][j * 4 + tt, :, :], in_=ks[:, :]),
                  reads=[ksB], writes=[Vd_B[store][j]])

    def fm_group(wt, wtB, ct, src_xT, src_B, n=CH):
        k = nb()
        mm_group(bank(k, n), [(wt[:, kc, ct * 128:(ct + 1) * 128], src_xT[:, kc, 0:n]) for kc in range(8)],
                 [wtB, src_B], bankB[k])
        return k

    def stage_A(jj):
        skew([ln_tile(x_oth[jj * CH + tt * 128:jj * CH + (tt + 1) * 128, :], 128, xTo, xToB, tt * 128)
              for tt in range(4)])
        kv_project(xTo, xToB, "oth", jj)

    for j in range(RUN_SLOTS):
        bmt, bmB = bmr.next()
        P.dma("sp", lambda e, bmt=bmt, j=j: e.dma_start(out=bmt[:, :], in_=bm_d[j, :, :]), writes=[bmB])

        if j == 0:
            stage_A(0)
            cast_batch1()

        skew([ln_tile(x_own[j * CH + tt * 128:j * CH + (tt + 1) * 128, :], 128, xT, xTB, tt * 128,
                      keep=(res[:, tt, :], resB[tt])) for tt in range(4)]
             + [ln_tile(x_halo[j * HALO:(j + 1) * HALO, :], HALO, xTh, xThB, 0)])
        if j == 0:
            cast_batch2()
        kv_project(xT, xTB, "own", j)

        wt, wtB = load_w(wbf_in, "in", 0, 8, 0)
        for g in range(4):
            k = fm_group(wt, wtB, g, xT, xTB)
            P.op("dve", lambda e, k=k, g=g: e.tensor_copy(out=uT[:, g, HALO:HALO + CH], in_=bank(k)),
                 reads=[bankB[k]], writes=[uTB[g]])
            k = fm_group(wt, wtB, g, xTh, xThB, n=HALO)
            P.op("dve", lambda e, k=k, g=g, j=j: e.tensor_scalar(out=uT[:, g, 0:HALO], in0=bank(k, HALO),
                                                                 scalar1=hv[:, j:j + 1], scalar2=None, op0=ALU.mult),
                 reads=[bankB[k], constB], writes=[uTB[g]])
        def proj_silu(grp, kind):
            wt, wtB = load_w(wbf_in, "in", 0, 8, grp * 512)
            for ct in range(4):
                k = fm_group(wt, wtB, ct, xT, xTB)
                P.op("act", lambda e, k=k, ct=ct, kind=kind: e.activation(out=sz[kind][:, ct, :], in_=bank(k),
                                                                          func=AF.Silu),
                     reads=[bankB[k]], writes=[szB[kind]])
        proj_silu(1, "pz")
        W = HALO + CH
        for g, w in enumerate((2, 4, 8, 16)):
            U = uT[:, g, :]
            P.op("pool", lambda e, U=U: e.tensor_tensor(out=pa[:, 1:W], in0=U[:, 1:W], in1=U[:, 0:W - 1], op=ALU.add),
                 reads=[uTB[g]], writes=[paB])
            cur, curB = pa, paB
            if w >= 4:
                P.op("pool", lambda e: e.tensor_tensor(out=pb[:, 3:W], in0=pa[:, 3:W], in1=pa[:, 1:W - 2], op=ALU.add),
                     reads=[paB], writes=[pbB])
                cur, curB = pb, pbB
            if w >= 8:
                P.op("pool", lambda e: e.tensor_tensor(out=pa[:, 7:W], in0=pb[:, 7:W], in1=pb[:, 3:W - 4], op=ALU.add),
                     reads=[pbB], writes=[paB])
                cur, curB = pa, paB
            if w >= 16:
                P.op("pool", lambda e: e.tensor_tensor(out=pb[:, 15:W], in0=pa[:, 15:W], in1=pa[:, 7:W - 8], op=ALU.add),
                     reads=[paB], writes=[pbB])
                cur, curB = pb, pbB
            pl, plB = pooled.next()
            P.op("dve", lambda e, cur=cur, U=U, w=w, pl=pl: e.scalar_tensor_tensor(
                out=pl[:, :], in0=cur[:, HALO:W], scalar=1.0 / w, in1=U[:, HALO:W], op0=ALU.mult, op1=ALU.subtract),
                reads=[curB, uTB[g]], writes=[plB])
            ic = invc[:, (j * 4 + g) * 16:(j * 4 + g + 1) * 16]
            P.op("pool", lambda e, cur=cur, ic=ic: e.tensor_tensor(out=cur[:, 0:16], in0=cur[:, HALO:HALO + 16], in1=ic, op=ALU.mult),
                 reads=[curB, constB, plB], writes=[curB])
            P.op("pool", lambda e, cur=cur, U=U, pl=pl: e.tensor_tensor(out=pl[:, 0:16], in0=cur[:, 0:16], in1=U[:, HALO:HALO + 16],
                                                                         op=ALU.subtract),
                 reads=[curB, uTB[g]], writes=[plB])
            k = nb()
            mm_group(bank(k), [(pwb[:, g * 128:(g + 1) * 128], pl[:, :])], [plB, constB], bankB[k])
            P.op("dve", lambda e, k=k, g=g: e.scalar_tensor_tensor(
                out=obr["pool"][:, g, :], in0=bank(k), scalar=psc[:, g:g + 1], in1=sz["pz"][:, g, :],
                op0=ALU.mult, op1=ALU.mult), reads=[bankB[k], szB["pz"], constB], writes=[obrB["pool"]])

        proj_silu(5, "dz")
        proj_silu(7, "mz")
        wt, wtB = load_w(wbf_in, "in", 0, 8, 2 * 512)
        for ct in range(4):
            k = fm_group(wt, wtB, ct, xT, xTB)
            P.op("dve", lambda e, k=k, ct=ct: e.tensor_scalar(out=QT[:, ct, :], in0=bank(k), scalar1=32.0 ** -0.5,
                                                              scalar2=None, op0=ALU.mult),
                 reads=[bankB[k]], writes=[QTB])
        wt, wtB = load_w(wbf_in, "in", 0, 8, 6 * 512)
        for ct in range(4):
            k = fm_group(wt, wtB, ct, xT, xTB)
            P.op("dve", lambda e, k=k, ct=ct: e.tensor_scalar(out=mqT[:, ct, :], in0=bank(k), scalar1=128.0 ** -0.5,
                                                              scalar2=None, op0=ALU.mult),
                 reads=[bankB[k]], writes=[mqB])

        if j == 0:
            mem_kv()
        for hd in range(4):
            kk = (nb() // 2) * 2
            bank_ctr[0] = kk + 2
            pmt, pmB = pm.next()
            for mt in range(2):
                mm_group(bank(kk + mt), [(mkT[:, hd, mt * 128:(mt + 1) * 128], mqT[:, hd, :])], [mkB, mqB],
                         bankB[kk + mt])
            P.op("act", lambda e, kk=kk, pmt=pmt: e.activation(
                out=pmt[:, :, :].rearrange("p a b -> p (a b)"), in_=ps[:, kk * 512:(kk + 2) * 512], func=AF.Exp),
                reads=[bankB[kk], bankB[kk + 1]], writes=[pmB])
            ko = nb()
            mm_group(bank(ko), [(mvb[:, mt, hd * 128:(hd + 1) * 128], pmt[:, mt, :]) for mt in range(2)],
                     [mvB_, pmB], bankB[ko])
            ksum = nb()
            mm_group(bank(ksum), [(ones[:, :], pmt[:, mt, :]) for mt in range(2)], [constB, pmB], bankB[ksum])
            f1, f1B = ftmp.next()
            f2, f2B = ftmp.next()
            P.op("dve", lambda e, ksum=ksum, f1=f1: e.reciprocal(out=f1[:, :], in_=bank(ksum)),
                 reads=[bankB[ksum]], writes=[f1B])
            P.op("dve", lambda e, ko=ko, f1=f1, f2=f2: e.tensor_tensor(out=f2[:, :], in0=bank(ko), in1=f1[:, :], op=ALU.mult),
                 reads=[bankB[ko], f1B], writes=[f2B])
            P.op("dve", lambda e, f2=f2, hd=hd: e.tensor_tensor(out=obr["mem"][:, hd, :], in0=f2[:, :],
                                                                in1=sz["mz"][:, hd, :], op=ALU.mult),
                 reads=[f2B, szB["mz"]], writes=[obrB["mem"]])

        entries = [("own", i, False) for i in range(j)] + [("oth", i, False) for i in range(j + 1)] + [("own", j, True)]
        step = [0]

        def live(st_, si, kt, h, diag):
            if diag:
                return True
            for r in range(2):
                cq = OWN[r][j]
                kc = (OWN[r] if st_ == "own" else OWN[1 - r])[si]
                if kc > cq:
                    continue
                if SLOPES[h] * (512 * (cq - kc) - 128 * kt - 127) <= SKIP_THRESH:
                    return True
            return False

        deferred = [None, 0]
        for hp in range(4):
            first = [True, True]
            pending = [None]
            for ei, (st_, si, diag) in enumerate(entries):
                lv = {(kt, hl): live(st_, si, kt, 2 * hp + hl, diag) for kt in range(4) for hl in range(2)}
                if not any(lv.values()):
                    continue
                kb, kB = kbufs.next()
                vb, vB = vbufs.next()
                P.dma("sp", lambda e, kb=kb, st_=st_, si=si, hp=hp: e.dma_start(
                    out=kb[:, :], in_=KTd[st_][hp, :, si * CH:(si + 1) * CH]), reads=[KTd_B[st_][si]], writes=[kB])
                P.dma("sp", lambda e, vb=vb, st_=st_, si=si, hp=hp: e.dma_start(
                    out=vb[:, :, :], in_=Vd[st_][si * 4:(si + 1) * 4, :, hp * 128:(hp + 1) * 128].rearrange("kt p c -> p kt c")),
                    reads=[Vd_B[st_][si]], writes=[vB])
                kts = (3, 2, 1, 0) if diag else (0, 1, 2, 3)
                for kt in kts:
                    lo = kt * 128 if diag else 0
                    last = diag and kt == 0
                    hls = [hl for hl in range(2) if lv[(kt, hl)]]
                    if not hls:
                        continue
                    pi = step[0] % 3
                    step[0] += 1
                    pt = PTs[pi]
                    for hl in hls:
                        h = 2 * hp + hl
                        sB = [bankB[2 * hl], bankB[2 * hl + 1]]

                        def fsc(e, kb=kb, kt=kt, hl=hl, hp=hp, lo=lo):
                            ins = None
                            for c in range(2):
                                g = 2 * hl + c
                                o = ps[:, g * 512 + lo:(g + 1) * 512]
                                ins = e.matmul(o, lhsT=kb[32 * g:32 * g + 32, kt * 128:(kt + 1) * 128],
                                               rhs=QT[32 * g:32 * g + 32, hp, lo:CH], start=True, stop=(hp >= 1),
                                               tile_position=(32 * g, 0))
                                if hp == 0:
                                    ins = e.matmul(o, lhsT=TK[32 * g:32 * g + 3, hp * 128:(hp + 1) * 128],
                                                   rhs=TQ[32 * g:32 * g + 3, hp * 512 + lo:(hp + 1) * 512],
                                                   start=False, stop=True, tile_position=(32 * g, 0))
                            return ins
                        P.op("pe", fsc, reads=[kB, QTB, constB], writes=sB)
                        bidx = (ei * 4 + kt) * 8 + h
                        P.op("act", lambda e, pt=pt, hl=hl, lo=lo, bidx=bidx, bmt=bmt: e.activation(
                            out=pt[:, hl, :].rearrange("p (c q) -> p c q", c=2)[:, :, lo:CH],
                            in_=ps[:, 2 * hl * 512:(2 * hl + 2) * 512].rearrange("p (c q) -> p c q", c=2)[:, :, lo:CH],
                            func=AF.Exp, bias=bmt[:, bidx:bidx + 1], scale=1.0),
                            reads=sB + [bmB], writes=[PTB[pi][hl]])
                        if diag:
                            P.op("dve", lambda e, pt=pt, hl=hl, lo=lo: e.tensor_tensor(
                                out=pt[:, hl, :].rearrange("p (c q) -> p c q", c=2)[:, :, lo:lo + 128],
                                in0=pt[:, hl, :].rearrange("p (c q) -> p c q", c=2)[:, :, lo:lo + 128],
                                in1=maskb[:, :].unsqueeze(1).to_broadcast([128, 2, 128]), op=ALU.mult),
                                reads=[PTB[pi][hl], constB], writes=[PTB[pi][hl]])

                    def fpv(e, vb=vb, kt=kt, pt=pt, lo=lo, fst=tuple(first), last=last, hls=tuple(hls)):
                        ins = None
                        for c in range(2):
                            for hl in hls:
                                rhs = pt[:, hl, c * 512 + lo:(c + 1) * 512]
                                e.matmul(ps[64 * hl:64 * hl + 64, (4 + c) * 512 + lo:(5 + c) * 512],
                                         lhsT=vb[:, kt, 64 * hl:64 * hl + 64], rhs=rhs, start=fst[hl], stop=last,
                                         tile_position=(0, 64 * hl))
                                ins = e.matmul(ps[64 * hl:64 * hl + 64, (6 + c) * 512 + lo:(7 + c) * 512],
                                               lhsT=ones[:, 0:64], rhs=rhs, start=fst[hl], stop=last,
                                               tile_position=(0, 64 * hl))
                        return ins
                    pv_args = (fpv, [vB, constB] + [PTB[pi][hl] for hl in hls], [bankB[4], bankB[5], bankB[6], bankB[7]])
                    for hl in hls:
                        first[hl] = False
                    if pending[0] is not None:
                        P.op("pe", pending[0][0], reads=pending[0][1], writes=pending[0][2])
                    pending[0] = pv_args
                    if deferred[0] is not None:
                        deferred[1] -= 1
                        if deferred[1] <= 0:
                            deferred[0]()
                            deferred[0] = None
            P.op("pe", pending[0][0], reads=pending[0][1], writes=pending[0][2])
            if deferred[0] is not None:
                deferred[0]()
                deferred[0] = None
            r0, r0B = ftmp.next()
            r1, r1B = ftmp.next()
            a0, a0B = ftmp.next()
            a1, a1B = ftmp.next()
            P.op("dve", lambda e, a0=a0: e.tensor_copy(out=a0[:, :], in_=bank(4)), reads=[bankB[4]], writes=[a0B])
            P.op("dve", lambda e, a1=a1: e.tensor_copy(out=a1[:, :], in_=bank(5)), reads=[bankB[5]], writes=[a1B])
            P.op("dve", lambda e, r0=r0: e.tensor_copy(out=r0[:, :], in_=bank(6)), reads=[bankB[6]], writes=[r0B])
            P.op("dve", lambda e, r1=r1: e.tensor_copy(out=r1[:, :], in_=bank(7)), reads=[bankB[7]], writes=[r1B])
            P.op("dve", lambda e, r0=r0: e.reciprocal(out=r0[:, :], in_=r0[:, :]), reads=[r0B], writes=[r0B])
            P.op("dve", lambda e, r1=r1: e.reciprocal(out=r1[:, :], in_=r1[:, :]), reads=[r1B], writes=[r1B])
            P.op("dve", lambda e, r0=r0, a0=a0: e.tensor_tensor(out=a0[:, :], in0=a0[:, :], in1=r0[:, :], op=ALU.mult),
                 reads=[a0B, r0B], writes=[a0B])
            P.op("dve", lambda e, r1=r1, a1=a1: e.tensor_tensor(out=a1[:, :], in0=a1[:, :], in1=r1[:, :], op=ALU.mult),
                 reads=[a1B, r1B], writes=[a1B])
            P.op("dve", lambda e, a0=a0, a1=a1: e.scalar_tensor_tensor(out=a0[:, :], in0=a1[:, :], scalar=neglam,
                                                                       in1=a0[:, :], op0=ALU.mult, op1=ALU.add),
                 reads=[a0B, a1B, constB], writes=[a0B])
            P.op("dve", lambda e, a0=a0, a1=a1: e.tensor_tensor(out=a1[:, :], in0=a0[:, :], in1=a0[:, :], op=ALU.mult),
                 reads=[a0B], writes=[a1B])

            def epi2(r0=r0, r0B=r0B, a0=a0, a0B=a0B, a1=a1, a1B=a1B, hp=hp):
                mm_group(bank(0), [(blk[:, :], a1[:, :])], [constB, a1B], bankB[0])
                P.op("dve", lambda e: e.tensor_scalar(out=r0[:, :], in0=bank(0), scalar1=64.0 * RMS_EPS, scalar2=None,
                                                      op0=ALU.add), reads=[bankB[0]], writes=[r0B])
                P.op("act", lambda e: e.activation(out=r0[:, :], in_=r0[:, :], func=AF.Sqrt), reads=[r0B], writes=[r0B])
                P.op("dve", lambda e: e.reciprocal(out=r0[:, :], in_=r0[:, :]), reads=[r0B], writes=[r0B])
                P.op("dve", lambda e: e.tensor_tensor(out=a0[:, :], in0=a0[:, :], in1=r0[:, :], op=ALU.mult),
                     reads=[a0B, r0B], writes=[a0B])
                P.op("dve", lambda e: e.scalar_tensor_tensor(
                    out=obr["diff"][:, hp, :], in0=a0[:, :], scalar=gsc[:, 0:1], in1=sz["dz"][:, hp, :],
                    op0=ALU.mult, op1=ALU.mult), reads=[a0B, szB["dz"], constB], writes=[obrB["diff"]])
            if hp < 3:
                deferred[0], deferred[1] = epi2, 6
            else:
                epi2()

        sched = {}
        if j + 1 < RUN_SLOTS:
            jn = j + 1
            tl = [ln_tile(x_oth[jn * CH + tt * 128:jn * CH + (tt + 1) * 128, :], 128, xTo, xToB, tt * 128)
                  for tt in range(4)]
            sched = {0: [tl[0][0]], 1: [tl[1][0]], 2: [tl[0][1], tl[2][0]], 3: [tl[1][1], tl[3][0]],
                     4: [tl[2][1]], 5: [tl[3][1]]}
        for hf in range(2):
            wg = []
            for n in range(3):
                wg.append(load_w(wbf_in, "in", 0, 8, (8 + 2 * n + hf) * 512))
            for dl in range(4):
                dt_ = 4 * hf + dl
                for piece in sched.get(dt_, ()):
                    piece()
                gs = []
                for n in range(3):
                    k = fm_group(wg[n][0], wg[n][1], dl, xT, xTB)
                    gt, gB = gsig.next()
                    P.op("act", lambda e, k=k, gt=gt, n=n, dt_=dt_: e.activation(
                        out=gt[:, :], in_=bank(k), func=AF.Sigmoid, bias=bg[:, n * 8 + dt_:n * 8 + dt_ + 1], scale=1.0),
                        reads=[bankB[k], constB], writes=[gB])
                    gs.append((gt, gB))
                if dl == 0:
                    wb = []
                    for n in range(3):
                        wbt, wbB = wbring.next()
                        wb.append(load_w(wbf_br, "br", n * 512, 4, hf * 512, wbt, wbB))
                ys = []
                for n, key in enumerate(("pool", "diff", "mem")):
                    k = nb()
                    mm_group(bank(k), [(wb[n][0][:, cc, dl * 128:(dl + 1) * 128], obr[key][:, cc, :]) for cc in range(4)],
                             [wb[n][1], obrB[key]], bankB[k])
                    ys.append(k)
                m, mB = ftmp.next()
                t, tB = ftmp.next()
                P.op("dve", lambda e, m=m, k=ys[0], g=gs[0][0]: e.tensor_tensor(out=m[:, :], in0=bank(k), in1=g[:, :], op=ALU.mult),
                     reads=[bankB[ys[0]], gs[0][1]], writes=[mB])
                P.op("dve", lambda e, t=t, k=ys[1], g=gs[1][0]: e.tensor_tensor(out=t[:, :], in0=bank(k), in1=g[:, :], op=ALU.mult),
                     reads=[bankB[ys[1]], gs[1][1]], writes=[tB])
                P.op("pool", lambda e, m=m, t=t: e.tensor_tensor(out=m[:, :], in0=m[:, :], in1=t[:, :], op=ALU.add),
                     reads=[mB, tB], writes=[mB])
                t2, t2B = ftmp.next()
                P.op("dve", lambda e, t2=t2, k=ys[2], g=gs[2][0]: e.tensor_tensor(out=t2[:, :], in0=bank(k), in1=g[:, :], op=ALU.mult),
                     reads=[bankB[ys[2]], gs[2][1]], writes=[t2B])
                P.op("pool", lambda e, m=m, t2=t2, dt_=dt_: e.tensor_tensor(out=merged[:, dt_, :], in0=m[:, :], in1=t2[:, :], op=ALU.add),
                     reads=[mB, t2B], writes=[mergedB])
        if j + 1 < RUN_SLOTS:
            kv_project(xTo, xToB, "oth", j + 1)
        wo = [load_w(wbf_out, "out", 0, 8, hf2 * 512) for hf2 in range(2)]
        def final_tile(tt, j=j):
            rb, rbB = rbuf.next()
            st, stB = stt.next()
            mv, mvB2 = mvr.next()

            def fa():
                for hf2 in range(2):
                    k = nb()
                    mm_group(bank(k), [(merged[:, kc, tt * 128:(tt + 1) * 128], wo[hf2][0][:, kc, :]) for kc in range(8)],
                             [mergedB, wo[hf2][1]], bankB[k])
                    P.op("dve", lambda e, k=k, hf2=hf2: e.tensor_tensor(
                        out=rb[:, hf2 * 512:(hf2 + 1) * 512], in0=bank(k), in1=res[:, tt, hf2 * 512:(hf2 + 1) * 512],
                        op=ALU.add), reads=[bankB[k], resB[tt]], writes=[rbB])
                P.op("dve", lambda e: e.bn_stats(out=st[:, 0:6], in_=rb[:, 0:512]), reads=[rbB], writes=[stB])
                P.op("dve", lambda e: e.bn_stats(out=st[:, 6:12], in_=rb[:, 512:1024]), reads=[rbB], writes=[stB])
                P.op("dve", lambda e: e.bn_aggr(out=mv[:, 0:2], in_=st[:, 0:12]), reads=[stB], writes=[mvB2])
                P.op("dve", lambda e: e.tensor_scalar(out=mv[:, 2:3], in0=mv[:, 1:2], scalar1=LN_EPS, scalar2=None,
                                                      op0=ALU.add), reads=[mvB2], writes=[mvB2])
                P.op("act", lambda e: e.activation(out=mv[:, 2:3], in_=mv[:, 2:3], func=AF.Sqrt),
                     reads=[mvB2], writes=[mvB2])
                P.op("dve", lambda e: e.reciprocal(out=mv[:, 2:3], in_=mv[:, 2:3]), reads=[mvB2], writes=[mvB2])
                P.op("dve", lambda e: e.scalar_tensor_tensor(out=mv[:, 3:4], in0=mv[:, 0:1], scalar=-1.0, in1=mv[:, 2:3],
                                                             op0=ALU.mult, op1=ALU.mult), reads=[mvB2], writes=[mvB2])
                P.op("act", lambda e: e.activation(out=rb[:, :], in_=rb[:, :], func=AF.Identity,
                                                   bias=mv[:, 3:4], scale=mv[:, 2:3]),
                     reads=[rbB, mvB2], writes=[rbB])

            def fb():
                P.op("dve", lambda e: e.tensor_tensor(out=rb[:, :], in0=rb[:, :], in1=gout[:, :], op=ALU.mult),
                     reads=[rbB, constB], writes=[rbB])
                P.op("pool", lambda e: e.tensor_tensor(out=rb[:, :], in0=rb[:, :], in1=bout[:, :], op=ALU.add),
                     reads=[rbB, constB], writes=[rbB])
                P.dma("pool", lambda e: e.dma_start(
                    out=out_own[j * CH + tt * 128:j * CH + (tt + 1) * 128, :], in_=rb[:, :]), reads=[rbB])
            return fa, fb
        skew([final_tile(tt) for tt in range(4)])

    Epool = engs["pool"]
    for q in dsems:
        for s in dsems[q]:
            if s["total"] > 0 and Epool.waited.get(s["key"], 0) < s["total"]:
                Epool.ops.append(("wait", s["sem"], s["total"]))

    with nc.Block() as block:
        @block.tensor
        def _(e):
            _run(e, engs["pe"])

        @block.scalar
        def _(e):
            _run(e, engs["act"])

        @block.vector
        def _(e):
            _run(e, engs["dve"])

        @block.gpsimd
        def _(e):
            _run(e, engs["pool"])

        @block.sync
        def _(e):
            _run(e, engs["sp"])

    es.close()
    return nc


def _host_tables():
    f = np.float32
    ident = np.eye(128, dtype=f)
    blk = np.zeros((128, 128), f)
    blk[:64, :64] = 1.0
    blk[64:, 64:] = 1.0
    kk = np.arange(128)[:, None]
    qq = np.arange(128)[None, :]
    mask = (kk <= qq).astype(f)
    slopes = np.array([2.0 ** (-(h + 1)) for h in range(8)], dtype=np.float64)
    tk = np.zeros((128, 4, 128), f)
    tq = np.zeros((128, 4, 512), f)
    qi = np.arange(512)
    for hp in range(4):
        for g in range(4):
            h = 2 * hp + g // 2
            tk[32 * g + 0, hp, :] = slopes[h] * np.arange(128)
            tk[32 * g + 1, hp, :] = 1.0
            tk[32 * g + 2, hp, :] = 1.0
            tq[32 * g + 0, hp, :] = 1.0
            tq[32 * g + 1, hp, :] = -slopes[h] * 128.0 * (qi // 128)
            tq[32 * g + 2, hp, :] = -slopes[h] * (qi % 128)
    return ident, blk, mask, tk.reshape(128, 512), tq.reshape(128, 2048), slopes


def _core_tables(r, slopes):
    f = np.float32
    own = OWN[r]
    oth = OWN[1 - r]
    bm = np.zeros((NSLOT, 128, 16, 4, 8), f)
    hv = np.zeros((128, 8), f)
    invc = np.zeros((128, 8, 4, 16), f)
    for j in range(NSLOT):
        c = own[j]
        entries = [own[i] for i in range(j)] + [oth[i] for i in range(j + 1)] + [c]
        for ei, kc in enumerate(entries):
            for kt in range(4):
                if ei == len(entries) - 1:
                    m = kt
                    valid = True
                else:
                    m = (512 * kc + 128 * kt - 512 * c) // 128
                    valid = kc < c
                for h in range(8):
                    if not valid:
                        bm[j, :, ei, kt, h] = NEG_BIG
                    elif h < 2:
                        bm[j, :, ei, kt, h] = slopes[h] * 128.0 * m
                    else:
                        bm[j, :, ei, kt, h] = slopes[h] * (np.arange(128) + 128.0 * m - 256.0)
        hv[:, j] = 0.0 if c == 0 else 1.0
        for g, w in enumerate((2, 4, 8, 16)):
            pos = 512 * c + np.arange(16)
            invc[:, j, g, :] = (1.0 / np.minimum(pos + 1, w)).astype(f)[None, :]
    return bm.reshape(NSLOT, 128, 512), hv, invc.reshape(128, 512)


_NC_CACHE = {}


def kernel(x, mem, ln_in_g, ln_in_b, w_in, b_gate, pool_w, pool_scale, lambda_q1, lambda_k1,
           lambda_q2, lambda_k2, diff_norm_g, w_mem_kv, w_branch, w_out, ln_out_g, ln_out_b):
    f = np.float32
    x = np.asarray(x, f)
    mem = np.asarray(mem, f)
    ident, blk, mask, tk, tq, slopes = _host_tables()
    rep = lambda v: np.ascontiguousarray(np.broadcast_to(np.asarray(v, f).reshape(1, -1), (128, np.asarray(v).size)))
    lam4 = np.concatenate([rep(lambda_q1), rep(lambda_k1), rep(lambda_q2), rep(lambda_k2)], axis=1)
    common = {
        "w_in": np.ascontiguousarray(np.asarray(w_in, f)[0]),
        "w_kv": np.ascontiguousarray(np.asarray(w_mem_kv, f)[0]),
        "w_br": np.ascontiguousarray(np.asarray(w_branch, f)[0].reshape(1536, D)),
        "w_out": np.ascontiguousarray(np.asarray(w_out, f)[0]),
        "poolw": np.ascontiguousarray(np.asarray(pool_w, f)[0].transpose(1, 0, 2).reshape(128, 512)),
        "g_in": rep(ln_in_g), "b_in": rep(ln_in_b),
        "g_fm": np.ascontiguousarray(np.asarray(ln_in_g, f).reshape(8, 128).T),
        "b_fm": np.ascontiguousarray(np.asarray(ln_in_b, f).reshape(8, 128).T),
        "g_out": rep(np.asarray(ln_out_g)[0]), "b_out": rep(np.asarray(ln_out_b)[0]),
        "bgate": np.ascontiguousarray(np.asarray(b_gate, f)[0].reshape(24, 128).T),
        "pscale": np.ascontiguousarray(np.asarray(pool_scale, f)[0].reshape(4, 128).T),
        "dng": np.ascontiguousarray(np.tile(np.asarray(diff_norm_g, f)[0], 2).reshape(128, 1)),
        "lam4": np.ascontiguousarray(lam4),
        "ident": ident, "blk": blk, "mask": mask, "tk": tk, "tq": tq,
    }
    in_maps = []
    for c in range(NCORES):
        b, r = c // 2, c % 2
        own, oth = OWN[r], OWN[1 - r]
        xb = x[b].reshape(16, CH, D)
        halo = np.zeros((NSLOT, HALO, D), f)
        for j, cj in enumerate(own):
            if cj > 0:
                halo[j] = x[b, cj * CH - HALO:cj * CH, :]
        bm, hv, invc = _core_tables(r, slopes)
        m = dict(common)
        m["x_own"] = np.ascontiguousarray(xb[own].reshape(NSLOT * CH, D))
        m["x_oth"] = np.ascontiguousarray(xb[oth].reshape(NSLOT * CH, D))
        m["x_halo"] = halo.reshape(NSLOT * HALO, D)
        m["memT"] = np.ascontiguousarray(mem[b].T)
        m["bm"] = bm
        m["hv"] = hv
        m["invc"] = invc
        in_maps.append(m)
    if "nc" not in _NC_CACHE:
        _NC_CACHE["nc"] = build()
    nc = _NC_CACHE["nc"]
    resu = run_bass_kernel_spmd(nc, in_maps, core_ids=list(range(NCORES)))
    out = np.zeros((BATCH, 16, CH, D), f)
    for c in range(NCORES):
        b, r = c // 2, c % 2
        o = np.asarray(resu.results[c]["out_own"], f).reshape(NSLOT, CH, D)
        for j, cj in enumerate(OWN[r]):
            out[b, cj] = o[j]
    return out.reshape(BATCH, SEQ, D)
```
